# Optimizing a Trainium2 kernel written in Bass

```python
import jax, jax.numpy as jnp
from jax import lax
import numpy as np

D_MODEL = 1024
BATCH = 8
SEQ = 2048
DEPTH = 1

CTX_LEN = 256
GRID_W = 64
CHUNK = 64
N_DIR = 2
GLA_HEADS = 4
GLA_DK = 64
GLA_DV = 128
GLA_RANK = 16
GLA_TAU = 16.0
GLA_QK = GLA_HEADS * GLA_DK
GLA_V = GLA_HEADS * GLA_DV
GDN_HEADS = 4
GDN_DK = 128
GDN_DV = 128
GDN_QK = GDN_HEADS * GDN_DK
GDN_V = GDN_HEADS * GDN_DV
CONV_K = 3
CONV_CH = 2 * GDN_QK + GDN_V
EPS = 1e-6
SPLIT_SIZES = (GLA_QK, GLA_QK, GLA_V, GLA_V, N_DIR * GLA_RANK,
               GDN_QK, GDN_QK, GDN_V, GDN_V, N_DIR * GDN_HEADS, N_DIR * GDN_HEADS,
               2 * D_MODEL)
D_IN = sum(SPLIT_SIZES)

kernel_name = 'hybrid_gla_gdn_prefix_dit_block'


def rmsnorm(x, g):
    xf = x.astype(jnp.float32)
    y = xf * lax.rsqrt(jnp.mean(xf * xf, axis=-1, keepdims=True) + EPS)
    return (y * g.astype(jnp.float32)).astype(x.dtype)


def l2norm(x):
    return x * lax.rsqrt(jnp.sum(x * x, axis=-1, keepdims=True) + EPS)


def rev(a):
    return jnp.flip(a, axis=1)


def split_cols(z):
    outs = []
    start = 0
    for size in SPLIT_SIZES:
        outs.append(z[..., start:start + size])
        start += size
    return outs


def conv_grid(u, w):
    b, t, ch = u.shape
    rows = t // GRID_W
    ug = u.reshape(b, rows, GRID_W, ch)
    out = lax.conv_general_dilated(ug, w[:, :, None, :].astype(u.dtype), window_strides=(1, 1), padding='SAME',
                                   dimension_numbers=('NHWC', 'HWIO', 'NHWC'), feature_group_count=ch)
    return out.reshape(b, t, ch)


def conv_seq(u, w):
    ch = u.shape[-1]
    return lax.conv_general_dilated(u, w[1][:, None, :].astype(u.dtype), window_strides=(1,), padding='SAME',
                                    dimension_numbers=('NWC', 'WIO', 'NWC'), feature_group_count=ch)


def to_chunks(a):
    b, t, h, d = a.shape
    return a.reshape(b, t // CHUNK, CHUNK, h, d).transpose(1, 0, 3, 2, 4)


def from_chunks(a):
    n, b, h, c, d = a.shape
    return a.transpose(1, 0, 3, 2, 4).reshape(b, n * c, h, d)


def gla_chunked(q, k, v, g, s0):
    qc, kc, vc, gc = to_chunks(q), to_chunks(k), to_chunks(v), to_chunks(g)
    bc = jnp.cumsum(gc, axis=3)
    causal = jnp.tril(jnp.ones((CHUNK, CHUNK), dtype=bool))

    def step(s, inp):
        qi, ki, vi, bi = inp
        diff = bi[:, :, :, None, :] - bi[:, :, None, :, :]
        dec = jnp.exp(jnp.where(causal[:, :, None], diff, -jnp.inf))
        att = jnp.einsum('bhtd,bhsd,bhtsd->bhts', qi, ki, dec)
        o = jnp.einsum('bhtd,bhde->bhte', qi * jnp.exp(bi), s) + jnp.einsum('bhts,bhse->bhte', att, vi)
        bl = bi[:, :, -1:, :]
        s_new = s * jnp.exp(bl[:, :, 0, :])[..., None] + jnp.einsum('bhsd,bhse->bhde', ki * jnp.exp(bl - bi), vi)
        return s_new, o

    s_fin, o = lax.scan(step, s0, (qc, kc, vc, bc))
    return from_chunks(o), s_fin


def gdn_chunked(q, k, v, g, beta, s0):
    qc, kc, vc = to_chunks(q), to_chunks(k), to_chunks(v)
    gc = to_chunks(g[..., None])[..., 0]
    bc = to_chunks(beta[..., None])
    gam = jnp.cumsum(gc, axis=-1)
    incl = jnp.tril(jnp.ones((CHUNK, CHUNK), dtype=bool))
    strict = jnp.tril(jnp.ones((CHUNK, CHUNK), dtype=bool), -1)
    dmask = jnp.exp(jnp.where(incl, gam[..., :, None] - gam[..., None, :], -jnp.inf))
    kb = kc * bc
    vb = vc * bc
    low = jnp.where(strict, jnp.einsum('nbhid,nbhjd->nbhij', kb, kc) * dmask, 0.0)
    eye = jnp.eye(CHUNK, dtype=low.dtype)
    tmat = lax.linalg.triangular_solve(eye + low, jnp.broadcast_to(eye, low.shape), left_side=True, lower=True)
    u = jnp.matmul(tmat, vb)
    w = jnp.matmul(tmat, kb * jnp.exp(gam)[..., None])
    aqk = jnp.einsum('nbhid,nbhjd->nbhij', qc, kc) * dmask

    def step(s, inp):
        qi, ki, ui, wi, ai, gi = inp
        v_new = ui - jnp.matmul(wi, s)
        o = jnp.matmul(qi * jnp.exp(gi)[..., None], s) + jnp.matmul(ai, v_new)
        gl = gi[..., -1]
        s_new = s * jnp.exp(gl)[..., None, None] + jnp.einsum('bhcd,bhce->bhde', ki * jnp.exp(gl[..., None] - gi)[..., None], v_new)
        return s_new, o

    s_fin, o = lax.scan(step, s0, (qc, kc, u, w, aqk, gam))
    return from_chunks(o), s_fin


def branch_inputs(h, w_in, gla_up, gla_ub, gdn_conv, gdn_a_log, gdn_dt_bias, on_grid):
    f32 = jnp.float32
    b, t, _ = h.shape
    z = jnp.matmul(h, w_in).astype(f32)
    aq, ak, av, ag, alr, bq, bk, bv, bg, bbeta, bdec, mg = split_cols(z)
    aq = aq.reshape(b, t, GLA_HEADS, GLA_DK) * (GLA_DK ** -0.5)
    ak = ak.reshape(b, t, GLA_HEADS, GLA_DK)
    av = av.reshape(b, t, GLA_HEADS, GLA_DV)
    alr = alr.reshape(b, t, N_DIR, GLA_RANK)
    alog = jax.nn.log_sigmoid(jnp.einsum('btnr,nrk->btnk', alr, gla_up.astype(f32)) + gla_ub.astype(f32)) / GLA_TAU
    alog = alog.reshape(b, t, N_DIR, GLA_HEADS, GLA_DK)
    qkv = jnp.concatenate([bq, bk, bv], axis=-1)
    qkv = jax.nn.silu(conv_grid(qkv, gdn_conv) if on_grid else conv_seq(qkv, gdn_conv))
    bq = l2norm(qkv[..., :GDN_QK].reshape(b, t, GDN_HEADS, GDN_DK)) * (GDN_DK ** -0.5)
    bk = l2norm(qkv[..., GDN_QK:2 * GDN_QK].reshape(b, t, GDN_HEADS, GDN_DK))
    bv = qkv[..., 2 * GDN_QK:].reshape(b, t, GDN_HEADS, GDN_DV)
    beta = jax.nn.sigmoid(bbeta.reshape(b, t, N_DIR, GDN_HEADS))
    glog = -jnp.exp(gdn_a_log.astype(f32)) * jax.nn.softplus(bdec.reshape(b, t, N_DIR, GDN_HEADS) + gdn_dt_bias.astype(f32))
    return aq, ak, av, ag, alog, bq, bk, bv, bg, beta, glog, mg


def run_mixers(aq, ak, av, alog, bq, bk, bv, beta, glog, sa_f, sa_b, sb_f, sb_b):
    oa_f, sa_f = gla_chunked(aq, ak, av, alog[:, :, 0], sa_f)
    oa_b, sa_b = gla_chunked(rev(aq), rev(ak), rev(av), rev(alog[:, :, 1]), sa_b)
    ob_f, sb_f = gdn_chunked(bq, bk, bv, glog[:, :, 0], beta[:, :, 0], sb_f)
    ob_b, sb_b = gdn_chunked(rev(bq), rev(bk), rev(bv), rev(glog[:, :, 1]), rev(beta[:, :, 1]), sb_b)
    return oa_f + rev(oa_b), ob_f + rev(ob_b), sa_f, sa_b, sb_f, sb_b


def branch_merge(oa, ob, ag, bg, mg, gla_onorm, gdn_onorm, w_gla_out, w_gdn_out, b_gate, w_o):
    b, t = oa.shape[0], oa.shape[1]
    ya = rmsnorm(oa, gla_onorm).reshape(b, t, GLA_V) * jax.nn.silu(ag)
    yb = rmsnorm(ob, gdn_onorm).reshape(b, t, GDN_V) * jax.nn.silu(bg)
    gates = jax.nn.sigmoid(mg + b_gate.astype(jnp.float32))
    merged = gates[..., :D_MODEL] * jnp.matmul(ya, w_gla_out) + gates[..., D_MODEL:] * jnp.matmul(yb, w_gdn_out)
    return jnp.matmul(merged, w_o)


def setup_inputs(seed: int = 0) -> dict:
    key = jax.random.key(seed)
    ks = jax.random.split(key, 24)

    def nrm(k, shape, scale):
        return jax.random.normal(k, shape, jnp.float32) * scale

    dt = jnp.exp(jax.random.uniform(ks[13], (DEPTH, N_DIR, GDN_HEADS), jnp.float32, np.log(1e-3), np.log(1e-1)))
    return {
        'x': nrm(ks[0], (BATCH, SEQ, D_MODEL), 1.0),
        'c': nrm(ks[1], (BATCH, D_MODEL), 1.0),
        'ctx': nrm(ks[2], (BATCH, CTX_LEN, D_MODEL), 1.0),
        'c_ctx': nrm(ks[3], (D_MODEL,), 1.0),
        'w_mod': nrm(ks[4], (DEPTH, D_MODEL, 3 * D_MODEL), 0.5 * D_MODEL ** -0.5),
        'b_mod': nrm(ks[5], (DEPTH, 3 * D_MODEL), 0.02),
        'norm_g': 1.0 + nrm(ks[6], (DEPTH, D_MODEL), 0.05),
        'w_in': nrm(ks[7], (DEPTH, D_MODEL, D_IN), D_MODEL ** -0.5),
        'gla_up': nrm(ks[8], (DEPTH, N_DIR, GLA_RANK, GLA_QK), GLA_RANK ** -0.5),
        'gla_ub': nrm(ks[9], (DEPTH, N_DIR, GLA_QK), 0.1),
        'gla_onorm': 1.0 + nrm(ks[10], (DEPTH, GLA_DV), 0.05),
        'gdn_conv': nrm(ks[11], (DEPTH, CONV_K, CONV_K, CONV_CH), 1.0 / CONV_K),
        'gdn_a_log': jnp.log(jax.random.uniform(ks[12], (DEPTH, N_DIR, GDN_HEADS), jnp.float32, 1.0, 16.0)),
        'gdn_dt_bias': dt + jnp.log(-jnp.expm1(-dt)),
        'gdn_onorm': 1.0 + nrm(ks[14], (DEPTH, GDN_DV), 0.05),
        'w_gla_out': nrm(ks[15], (DEPTH, GLA_V, D_MODEL), GLA_V ** -0.5),
        'w_gdn_out': nrm(ks[16], (DEPTH, GDN_V, D_MODEL), GDN_V ** -0.5),
        'b_gate': nrm(ks[17], (DEPTH, 2 * D_MODEL), 0.1),
        'w_o': nrm(ks[18], (DEPTH, D_MODEL, D_MODEL), D_MODEL ** -0.5),
        'final_g': 1.0 + nrm(ks[19], (D_MODEL,), 0.05),
    }


def reference(x, c, ctx, c_ctx, w_mod, b_mod, norm_g, w_in, gla_up, gla_ub, gla_onorm, gdn_conv,
              gdn_a_log, gdn_dt_bias, gdn_onorm, w_gla_out, w_gdn_out, b_gate, w_o, final_g):
    dtype = x.dtype
    b = x.shape[0]
    f32 = jnp.float32
    for l in range(DEPTH):
        mod_x = jnp.matmul(jax.nn.silu(c), w_mod[l]) + b_mod[l]
        mod_c = jnp.matmul(jax.nn.silu(c_ctx), w_mod[l]) + b_mod[l]
        sh_x, sc_x, gt_x = mod_x[:, :D_MODEL], mod_x[:, D_MODEL:2 * D_MODEL], mod_x[:, 2 * D_MODEL:]
        sh_c, sc_c, gt_c = mod_c[:D_MODEL], mod_c[D_MODEL:2 * D_MODEL], mod_c[2 * D_MODEL:]
        hx = rmsnorm(x, norm_g[l]) * (1.0 + sc_x[:, None, :]) + sh_x[:, None, :]
        hc = rmsnorm(ctx, norm_g[l]) * (1.0 + sc_c) + sh_c

        (caq, cak, cav, cag, calog, cbq, cbk, cbv, cbg, cbeta, cglog, cmg) = branch_inputs(
            hc, w_in[l], gla_up[l], gla_ub[l], gdn_conv[l], gdn_a_log[l], gdn_dt_bias[l], False)
        za = jnp.zeros((b, GLA_HEADS, GLA_DK, GLA_DV), f32)
        zb = jnp.zeros((b, GDN_HEADS, GDN_DK, GDN_DV), f32)
        coa, cob, sa_f, sa_b, sb_f, sb_b = run_mixers(caq, cak, cav, calog, cbq, cbk, cbv, cbeta, cglog, za, za, zb, zb)

        (aq, ak, av, ag, alog, bq, bk, bv, bg, beta, glog, mg) = branch_inputs(
            hx, w_in[l], gla_up[l], gla_ub[l], gdn_conv[l], gdn_a_log[l], gdn_dt_bias[l], True)
        oa, ob, _, _, _, _ = run_mixers(aq, ak, av, alog, bq, bk, bv, beta, glog, sa_f, sa_b, sb_f, sb_b)
        out_x = branch_merge(oa, ob, ag, bg, mg, gla_onorm[l], gdn_onorm[l], w_gla_out[l], w_gdn_out[l], b_gate[l], w_o[l])
        if l < DEPTH - 1:
            out_c = branch_merge(coa, cob, cag, cbg, cmg, gla_onorm[l], gdn_onorm[l], w_gla_out[l], w_gdn_out[l], b_gate[l], w_o[l])
            ctx = ctx + (gt_c * out_c).astype(dtype)
        x = x + (gt_x[:, None, :] * out_x).astype(dtype)
    return rmsnorm(x, final_g)
```

```python
import numpy as np
from contextlib import ExitStack
import concourse.bass as bass
import concourse.mybir as mybir
from concourse.bass_utils import run_bass_kernel_spmd

F32 = mybir.dt.float32
BF16 = mybir.dt.bfloat16
AF = mybir.ActivationFunctionType
ALU = mybir.AluOpType
AX = mybir.AxisListType

D = 1024
T = 2048
TC = 256
TT = T + TC
NTILE = TT // 128
DIN = 5680
EPS = 1e-6
NEG = -30000.0

C_AQ, C_AK, C_AV, C_AG, C_ALR = 0, 256, 512, 1024, 1536
C_BQ, C_BK, C_BV, C_BG, C_BB, C_BD, C_MG = 1568, 2080, 2592, 3104, 3616, 3624, 3632


class Sched:
    ENGS = ("pe", "act", "dve", "pool", "sp")

    def __init__(self, nc, st):
        self.nc = nc
        self.st = st
        self.eng = {"pe": nc.tensor, "act": nc.scalar, "dve": nc.vector, "pool": nc.gpsimd, "sp": nc.sync}
        self.esem = {e: st.enter_context(nc.semaphore("es_" + e)) for e in self.ENGS}
        self.cnt = {e: 0 for e in self.ENGS}
        self.waited = {e: {} for e in self.ENGS}
        self.lastw = {}
        self.readers = {}
        self.dsem = {}
        self.sems = {}
        for e in self.ENGS:
            self.sems[self.esem[e].name] = self.esem[e]
        self.ninst = 0

    def _waits(self, e, evs):
        for (sn, v) in evs:
            if self.waited[e].get(sn, 0) >= v:
                continue
            self.waited[e][sn] = v
            self.eng[e].wait_ge(self.sems[sn], v)

    def _deps(self, e, rk, wk):
        evs = []
        own = self.esem[e].name
        for k in rk:
            ev = self.lastw.get(k)
            if ev is not None:
                evs.append(ev)
        for k in wk:
            ev = self.lastw.get(k)
            if ev is not None:
                evs.append(ev)
            for ev in self.readers.get(k, ()):
                evs.append(ev)
        if e == "pe":
            evs = [ev for ev in evs if ev[0] != own]
        return evs

    def _commit(self, ev, rk, wk):
        for k in rk:
            lst = self.readers.setdefault(k, [])
            lst[:] = [x for x in lst if x[0] != ev[0]] + [ev]
        for k in wk:
            self.lastw[k] = ev
            self.readers[k] = []

    @staticmethod
    def keys(aps):
        out = []
        for a in aps:
            if a is None:
                continue
            if isinstance(a, str):
                out.append(a)
            elif hasattr(a, "tensor"):
                out.append(a.tensor.name)
            else:
                out.append(a.name)
        return out

    def op(self, e, fn, r=(), w=()):
        rk, wk = self.keys(r), self.keys(w)
        wk = wk + [k for k in rk if k.startswith("ps") and k not in wk]
        self._waits(e, self._deps(e, rk, wk))
        ins = fn(self.eng[e])
        ins.then_inc(self.esem[e], 1)
        self.cnt[e] += 1
        self.ninst += 1
        self._commit((self.esem[e].name, self.cnt[e]), rk, wk)

    def dma(self, e, out, in_, semkey, r=(), w=(), **kw):
        rk, wk = self.keys(r), self.keys(w)
        self._waits(e, self._deps(e, rk, wk))
        if semkey not in self.dsem:
            s = self.st.enter_context(self.nc.semaphore("ds_%d" % len(self.dsem)))
            self.dsem[semkey] = [s, 0]
            self.sems[s.name] = s
        s = self.dsem[semkey]
        self.eng[e].dma_start(out=out, in_=in_, **kw).then_inc(s[0], 16)
        s[1] += 16
        self.ninst += 1
        self._commit((s[0].name, s[1]), rk, wk)

    def finish(self, e="sp"):
        for k, (s, v) in self.dsem.items():
            self.eng[e].wait_ge(s, v)
        for e2 in self.ENGS:
            if e2 != e and self.cnt[e2] > 0:
                self.eng[e].wait_ge(self.esem[e2], self.cnt[e2])


def _barrier(S):
    tgt = [(S.esem[e].name, S.cnt[e]) for e in S.ENGS if S.cnt[e] > 0]
    tgt += [(s.name, v) for (s, v) in S.dsem.values()]
    for e in S.ENGS:
        S._waits(e, [ev for ev in tgt if ev[0] != S.esem[e].name])


def run_chains(chains, delays=None, strides=None):
    alive = list(chains)
    delays = dict(zip(map(id, chains), delays or [0] * len(chains)))
    strides = dict(zip(map(id, chains), strides or [1] * len(chains)))
    rnd = 0
    while alive:
        rnd += 1
        for g_ in list(alive):
            if delays[id(g_)] > 0:
                delays[id(g_)] -= 1
                continue
            if rnd % strides[id(g_)] != 0 and any(strides[id(o_)] == 1 for o_ in alive):
                continue
            try:
                next(g_)
            except StopIteration:
                alive.remove(g_)


def build_program(debug=0):
    nc = bass.Bass("TRN2", target_bir_lowering=False)
    dk = "ExternalOutput" if debug else "Internal"

    def din(name, shape):
        return nc.dram_tensor(name, list(shape), F32, kind="ExternalInput").ap()

    x_d = din("x", [T, D]); c_d = din("c", [D]); ctx_d = din("ctx", [TC, D]); cctx_d = din("c_ctx", [D])
    wmod_d = din("w_mod", [D, 3 * D]); bmod_d = din("b_mod", [3 * D]); ng_d = din("norm_g", [D])
    win_d = din("w_in", [D, DIN]); gup_d = din("gla_up", [2, 16, 256]); gub_d = din("gla_ub", [2, 256])
    gon_d = din("gla_onorm", [128]); conv_d = din("gdn_conv", [9, 1536]); alog_d = din("gdn_a_log", [8])
    dtb_d = din("gdn_dt_bias", [8]); don_d = din("gdn_onorm", [128]); wga_d = din("w_gla_out", [512, D])
    wgd_d = din("w_gdn_out", [512, D]); bg_d = din("b_gate", [2 * D]); wo_d = din("w_o", [D, D])
    fg_d = din("final_g", [D])
    out_d = nc.dram_tensor("out", [T, D], F32, kind="ExternalOutput").ap()

    def scr(name, shape, dt):
        return nc.dram_tensor(name, list(shape), dt, kind=dk).ap()

    s_qTa = scr("s_qTa", [256, TT], BF16); s_kTa = scr("s_kTa", [256, TT], BF16)
    s_kta = scr("s_kta", [TT, 256], BF16); s_vta = scr("s_vta", [TT, 512], BF16)
    s_alr = scr("s_alr", [32, TT], F32); s_zg = scr("s_zg", [1536, TT], BF16)
    s_bd = scr("s_bd", [TT, 16], F32)
    s_ga = scr("s_ga", [512, T], BF16); s_gb = scr("s_gb", [512, T], BF16); s_gm = scr("s_gm", [2048, T], BF16)
    s_qTb = scr("s_qTb", [512, TT], BF16); s_kTb = scr("s_kTb", [512, TT], BF16)
    s_ktb = scr("s_ktb", [TT, 512], BF16); s_vtb = scr("s_vtb", [TT, 512], BF16)
    if debug:
        s_oacc = scr("s_oacc", [128, 16 * 8 * 128], F32)

    with ExitStack() as st:
        S = Sched(nc, st)

        def sbt(stk, name, shape, dt=F32):
            return stk.enter_context(nc.sbuf_tensor(name, list(shape), dt))

        def sb(name, shape, dt=F32):
            return sbt(st, name, shape, dt)

        ident = sb("ident", [128, 128]); ones = sb("ones", [128, 128]); identb = sb("identb", [128, 128], BF16)
        S.op("pool", lambda e: e.memset(ones[:], 1.0), w=[ones])
        S.op("pool", lambda e: e.memset(ident[:], 1.0), w=[ident])
        S.op("pool", lambda e: e.affine_select(out=ident[:], in_=ident[:], pattern=[[1, 128]], compare_op=ALU.is_equal,
                                               fill=0.0, base=0, channel_multiplier=-1), r=[ident], w=[ident])
        S.op("pool", lambda e: e.tensor_copy(out=identb[:], in_=ident[:]), r=[ident], w=[identb])


        modc = sb("modc", [128, 16, 2]); Acol = sb("Acol", [128, 8, 2])
        gtbc = sb("gtbc", [128, 1024]); fgbc = sb("fgbc", [128, 1024])

        with ExitStack() as s1:
            PS = [s1.enter_context(nc.psum_tensor("psa%d" % i, [128, 512], F32)) for i in range(7)]
            win = sbt(s1, "win", [128, 8, DIN], BF16)
            for kc in range(8):
                S.dma("pool", win[:, kc, :], win_d[kc * 128:(kc + 1) * 128, :], "win", w=[win])
            cT = sbt(s1, "cT", [128, 8, 2])
            S.dma("sp", cT[:, :, 0], c_d.rearrange("(k p) -> p k", p=128), "cT", w=[cT], allow_slow_non_contiguous=True)
            S.dma("sp", cT[:, :, 1], cctx_d.rearrange("(k p) -> p k", p=128), "cT", w=[cT],
                  allow_slow_non_contiguous=True)
            scT = sbt(s1, "scT", [128, 8, 2])
            S.op("act", lambda e: e.activation(out=scT[:], in_=cT[:], func=AF.Silu), r=[cT], w=[scT])
            bmodc = sbt(s1, "bmodc", [128, 16])
            S.dma("sp", bmodc[:], bmod_d[0:2048].rearrange("(m p) -> p m", p=128), "bmodc", w=[bmodc],
                  allow_slow_non_contiguous=True)
            ngc = sbt(s1, "ngc", [128, 8])
            S.dma("sp", ngc[:], ng_d.rearrange("(k p) -> p k", p=128), "ngc", w=[ngc], allow_slow_non_contiguous=True)
            bgc = sbt(s1, "bgc", [128, 16])
            S.dma("sp", bgc[:], bg_d.rearrange("(m p) -> p m", p=128), "bgc", w=[bgc], allow_slow_non_contiguous=True)
            rowbuf = sbt(s1, "rowbuf", [1, 2, 1024])
            S.dma("sp", rowbuf[0:1, 0, :], bmod_d[2048:3072].rearrange("(o n) -> o n", o=1), "rowbuf", w=[rowbuf])
            S.dma("sp", rowbuf[0:1, 1, :], fg_d.rearrange("(o n) -> o n", o=1), "rowbuf", w=[rowbuf])
            wm = [sbt(s1, "wm%d" % i, [128, 8, 512]) for i in range(2)]
            gtrow = sbt(s1, "gtrow", [1, 1024])
            for j in range(6):
                wb = wm[j % 2]
                S.dma("sp", wb[:], wmod_d[:, j * 512:(j + 1) * 512].rearrange("(k p) n -> p k n", p=128),
                      "wm%d" % (j % 2), w=[wb])
                if j < 4:
                    for mi in range(4):
                        m = j * 4 + mi
                        pt = PS[m % 2]
                        for kc in range(8):
                            S.op("pe", lambda e, kc=kc, mi=mi, wb=wb, pt=pt: e.matmul(
                                pt[:, 0:2], lhsT=wb[:, kc, mi * 128:(mi + 1) * 128], rhs=scT[:, kc, :],
                                start=(kc == 0), stop=(kc == 7)), r=[wb, scT], w=[pt])
                        S.op("dve", lambda e, m=m, pt=pt: e.tensor_scalar(
                            out=modc[:, m, :], in0=pt[:, 0:2], scalar1=bmodc[:, m:m + 1], scalar2=None, op0=ALU.add),
                            r=[pt, bmodc], w=[modc])
                else:
                    pt = PS[2 + (j % 2)]
                    for kc in range(8):
                        S.op("pe", lambda e, kc=kc, wb=wb, pt=pt: e.matmul(
                            pt[0:1, :], lhsT=scT[:, kc, 0:1], rhs=wb[:, kc, :], start=(kc == 0), stop=(kc == 7)),
                            r=[wb, scT], w=[pt])
                    S.op("dve", lambda e, j=j, pt=pt: e.tensor_tensor(
                        out=gtrow[0:1, (j - 4) * 512:(j - 3) * 512], in0=pt[0:1, :],
                        in1=rowbuf[0:1, 0, (j - 4) * 512:(j - 3) * 512], op=ALU.add), r=[pt, rowbuf], w=[gtrow])
            for h in range(2):
                pt = PS[4 + h]
                S.op("pe", lambda e, h=h, pt=pt: e.matmul(pt[:], lhsT=ones[0:1, :], rhs=gtrow[0:1, h * 512:(h + 1) * 512],
                                                         start=True, stop=True), r=[ones, gtrow], w=[pt])
                S.op("act", lambda e, h=h, pt=pt: e.activation(out=gtbc[:, h * 512:(h + 1) * 512], in_=pt[:],
                                                               func=AF.Identity), r=[pt], w=[gtbc])
                pt2 = PS[h]
                S.op("pe", lambda e, h=h, pt2=pt2: e.matmul(pt2[:], lhsT=ones[0:1, :],
                                                            rhs=rowbuf[0:1, 1, h * 512:(h + 1) * 512],
                                                            start=True, stop=True), r=[ones, rowbuf], w=[pt2])
                S.op("act", lambda e, h=h, pt2=pt2: e.activation(out=fgbc[:, h * 512:(h + 1) * 512], in_=pt2[:],
                                                                 func=AF.Identity), r=[pt2], w=[fgbc])
            S.op("dve", lambda e: e.tensor_scalar(out=Acol[:], in0=modc[:, 8:16, :], scalar1=1.0, scalar2=None,
                                                  op0=ALU.add), r=[modc], w=[Acol])
            for i in range(2):
                S.op("dve", lambda e, i=i: e.tensor_tensor(out=Acol[:, :, i], in0=Acol[:, :, i], in1=ngc[:],
                                                           op=ALU.mult), r=[Acol, ngc], w=[Acol])

            xt = [sbt(s1, "xt%d" % i, [128, 1024]) for i in range(2)]
            xn = [sbt(s1, "xn%d" % i, [128, 1024]) for i in range(2)]
            junk = sbt(s1, "junk", [128, 1024], BF16)
            ssq = sbt(s1, "ssq", [128, 4])
            hT = [sbt(s1, "hT%d" % i, [128, 8, 512], BF16) for i in range(2)]
            stg = [sbt(s1, "stg%d" % i, [128, 512], BF16) for i in range(4)]
            stg32 = [sbt(s1, "stgf%d" % i, [128, 512]) for i in range(2)]
            groups = [(0, 2)] + [(2 + 4 * g, 4) for g in range(4)]
            nstg = [0, 0, 0, 0]
            def prep_gen(gi):
                tile0, ntl = groups[gi]
                hb = hT[gi % 2]
                ci = 1 if gi == 0 else 0

                def stage1(tl):
                    tile = tile0 + tl
                    xb_, xnb = xt[tile % 2], xn[tile % 2]
                    src = ctx_d[tile * 128:(tile + 1) * 128, :] if tile < 2 else x_d[(tile - 2) * 128:(tile - 1) * 128, :]
                    S.dma("sp", xb_[:], src, xb_.name, w=[xb_])
                    S.op("act", lambda e: e.activation(out=junk[:], in_=xb_[:], func=AF.Square,
                                                       accum_out=ssq[:, 0:1]), r=[xb_], w=[junk, ssq])
                    S.op("dve", lambda e: e.tensor_scalar(out=ssq[:, 1:2], in0=ssq[:, 0:1], scalar1=1.0 / D, scalar2=EPS,
                                                          op0=ALU.mult, op1=ALU.add), r=[ssq], w=[ssq])
                    S.op("act", lambda e: e.activation(out=ssq[:, 2:3], in_=ssq[:, 1:2], func=AF.Ln), r=[ssq], w=[ssq])
                    S.op("act", lambda e: e.activation(out=ssq[:, 3:4], in_=ssq[:, 2:3], func=AF.Exp, scale=-0.5),
                         r=[ssq], w=[ssq])
                    S.op("dve", lambda e: e.tensor_scalar(out=xnb[:], in0=xb_[:], scalar1=ssq[:, 3:4],
                                                          scalar2=None, op0=ALU.mult), r=[xb_, ssq], w=[xnb])

                def stage2(tl):
                    tile = tile0 + tl
                    xnb = xn[tile % 2]
                    for kc in range(8):
                        pt = PS[kc // 4]
                        S.op("pe", lambda e, kc=kc, pt=pt: e.transpose(
                            out=pt[:, (kc % 4) * 128:(kc % 4 + 1) * 128], in_=xnb[:, kc * 128:(kc + 1) * 128],
                            identity=ident[:]), r=[xnb, ident], w=[pt])
                    for kc in range(8):
                        pt = PS[kc // 4]
                        S.op("act", lambda e, kc=kc, pt=pt: e.activation(
                            out=hb[:, kc, tl * 128:(tl + 1) * 128], in_=pt[:, (kc % 4) * 128:(kc % 4 + 1) * 128],
                            func=AF.Identity, scale=Acol[:, kc, ci:ci + 1], bias=modc[:, kc, ci:ci + 1]),
                            r=[pt, Acol, modc], w=[hb])
                seq = [(stage1, 0)]
                for tl in range(ntl):
                    if tl + 1 < ntl:
                        seq.append((stage1, tl + 1))
                    seq.append((stage2, tl))
                for fn_, a_ in seq:
                    fn_(a_)
                    yield

            fm = []
            for i in range(2):
                fm.append((C_AQ + i * 128, 128, s_qTa, i * 128, "cp", None, True))
                fm.append((C_AK + i * 128, 128, s_kTa, i * 128, "cp", None, True))
            fm.append((C_ALR, 32, s_alr, 0, "cp32", None, True))
            for i in range(12):
                fm.append((C_BQ + i * 128, 128, s_zg, i * 128, "cp", None, True))
            for i in range(4):
                fm.append((C_AG + i * 128, 128, s_ga, i * 128, "silu", None, False))
                fm.append((C_BG + i * 128, 128, s_gb, i * 128, "silu", None, False))
            for i in range(16):
                fm.append((C_MG + i * 128, 128, s_gm, i * 128, "sig", i, False))

            def fm_block(gi, spec):
                (c0, ncol, dst, r0, kind, bi, withctx) = spec
                tile0, ntl = groups[gi]
                hb = hT[gi % 2]; ntok = ntl * 128; tok0 = tile0 * 128
                pt = PS[2 + nstg[2] % 3]; nstg[2] += 1
                for kc in range(8):
                    S.op("pe", lambda e, kc=kc: e.matmul(
                        pt[0:ncol, 0:ntok], lhsT=win[:, kc, c0:c0 + ncol], rhs=hb[:, kc, 0:ntok],
                        start=(kc == 0), stop=(kc == 7)), r=[win, hb], w=[pt])
                if kind == "cp32":
                    sg = stg32[nstg[1] % 2]; nstg[1] += 1
                    S.op("dve", lambda e: e.tensor_copy(out=sg[0:ncol, 0:ntok], in_=pt[0:ncol, 0:ntok]), r=[pt], w=[sg])
                else:
                    sg = stg[nstg[0] % 4]; nstg[0] += 1
                    if kind == "cp":
                        S.op("dve", lambda e: e.tensor_copy(out=sg[0:ncol, 0:ntok], in_=pt[0:ncol, 0:ntok]),
                             r=[pt], w=[sg])
                    elif kind == "silu":
                        S.op("act", lambda e: e.activation(out=sg[0:ncol, 0:ntok], in_=pt[0:ncol, 0:ntok],
                                                           func=AF.Silu), r=[pt], w=[sg])
                    else:
                        S.op("act", lambda e: e.activation(out=sg[0:ncol, 0:ntok], in_=pt[0:ncol, 0:ntok],
                                                           func=AF.Sigmoid, bias=bgc[:, bi:bi + 1]),
                             r=[pt, bgc], w=[sg])
                cofs = tok0 if withctx else tok0 - TC
                S.dma("sp", dst[r0:r0 + ncol, cofs:cofs + ntok], sg[0:ncol, 0:ntok], sg.name, r=[sg],
                      w=[dst.tensor.name + "_w"])

            def tm_block(gi, tl, spec):
                (c0, ncol, dst, f32) = spec
                tile0, ntl = groups[gi]
                hb = hT[gi % 2]; tile = tile0 + tl
                pt = PS[5 + nstg[3] % 2]; nstg[3] += 1
                for kc in range(8):
                    S.op("pe", lambda e, kc=kc: e.matmul(
                        pt[:, 0:ncol], lhsT=hb[:, kc, tl * 128:(tl + 1) * 128], rhs=win[:, kc, c0:c0 + ncol],
                        start=(kc == 0), stop=(kc == 7)), r=[win, hb], w=[pt])
                if f32:
                    sg = stg32[nstg[1] % 2]; nstg[1] += 1
                else:
                    sg = stg[nstg[0] % 4]; nstg[0] += 1
                S.op("act", lambda e: e.activation(out=sg[:, 0:ncol], in_=pt[:, 0:ncol], func=AF.Identity),
                     r=[pt], w=[sg])
                S.dma("sp", dst[tile * 128:(tile + 1) * 128, :], sg[:, 0:ncol], sg.name, r=[sg],
                      w=[dst.tensor.name + "_w"])

            for _ in prep_gen(0):
                pass
            for gi, (tile0, ntl) in enumerate(groups):
                nxt = prep_gen(gi + 1) if gi + 1 < len(groups) else None
                blocks = [(fm_block, (gi, sp_)) for sp_ in fm if (gi > 0 or sp_[6])]
                for tl in range(ntl):
                    for sp_ in ((C_AK, 256, s_kta, False), (C_AV, 512, s_vta, False), (C_BB, 16, s_bd, True)):
                        blocks.append((tm_block, (gi, tl, sp_)))
                every = max(1, (len(blocks) - 2) // 8)
                for bi_, (fn_, args_) in enumerate(blocks):
                    fn_(*args_)
                    if nxt is not None and bi_ % every == every - 1:
                        next(nxt, None)
                if nxt is not None:
                    for _ in nxt:
                        pass
            _barrier(S)

        if debug == 1:
            S.finish()
            return nc
        def mm(out, lhsT, rhs, st_=True, sp_=True):
            S.op("pe", lambda e: e.matmul(out, lhsT=lhsT, rhs=rhs, start=st_, stop=sp_), r=[lhsT, rhs], w=[out])

        def tr(out, in_, idn):
            S.op("pe", lambda e: e.transpose(out=out, in_=in_, identity=idn), r=[in_, idn], w=[out])

        def act(out, in_, func, bias=None, scale=None):
            kw = {}
            rr = [in_]
            if bias is not None:
                kw["bias"] = bias
                if not isinstance(bias, float):
                    rr.append(bias)
            if scale is not None:
                kw["scale"] = scale
                if not isinstance(scale, float):
                    rr.append(scale)
            S.op("act", lambda e: e.activation(out=out, in_=in_, func=func, **kw), r=rr, w=[out])

        def tt(eng, out, in0, in1, op):
            S.op(eng, lambda e: e.tensor_tensor(out=out, in0=in0, in1=in1, op=op), r=[in0, in1], w=[out])

        def ts(eng, out, in0, s1, op0, s2=None, op1=None):
            rr = [in0] + [v for v in (s1, s2) if v is not None and not isinstance(v, float)]
            if op1 is None:
                S.op(eng, lambda e: e.tensor_scalar(out=out, in0=in0, scalar1=s1, scalar2=None, op0=op0), r=rr, w=[out])
            else:
                S.op(eng, lambda e: e.tensor_scalar(out=out, in0=in0, scalar1=s1, scalar2=s2, op0=op0, op1=op1),
                     r=rr, w=[out])

        def stt(out, in0, scalar, in1, op0, op1):
            rr = [in0, in1] + ([] if isinstance(scalar, float) else [scalar])
            S.op("dve", lambda e: e.scalar_tensor_tensor(out=out, in0=in0, scalar=scalar, in1=in1, op0=op0, op1=op1),
                 r=rr, w=[out])

        def cp(eng, out, in_):
            if eng == "act":
                act(out, in_, AF.Identity)
            else:
                S.op(eng, lambda e: e.tensor_copy(out=out, in_=in_), r=[in_], w=[out])

        def rsqrt_(out, in_, eps=0.0):
            act(out, in_, AF.Ln, bias=eps)
            act(out, out, AF.Exp, scale=-0.5)

        with ExitStack() as s2:
            PS = [s2.enter_context(nc.psum_tensor("psd%d" % i, [128, 512], F32)) for i in range(6)]
            PSBs = [s2.enter_context(nc.psum_tensor("psbf%d" % i, [128, 1024], BF16)) for i in range(2)]
            cwc = sbt(s2, "cwc", [128, 12, 9])
            for cb in range(12):
                S.dma("sp", cwc[:, cb, :], conv_d[:, cb * 128:(cb + 1) * 128].rearrange("t p -> p t"), "cwc", w=[cwc],
                      allow_slow_non_contiguous=True)

            def d_chain(par):
                n = "d%d" % par
                pl = sbt(s2, "padl" + n, [128, 34, 66], BF16); pc = sbt(s2, "padc" + n, [128, 258], BF16)
                S.op("pool", lambda e: e.memset(pl[:], 0.0), w=[pl])
                S.op("pool", lambda e: e.memset(pc[:], 0.0), w=[pc])
                dgb = sbt(s2, "dg" + n, [128, 9, 128], BF16)
                s32 = sbt(s2, "s32" + n, [128, TT]); sq = sbt(s2, "sq" + n, [128, 512]); ssb = sbt(s2, "ssb" + n, [128, TT])
                nbb = sbt(s2, "nb" + n, [128, TT], BF16)
                tstg = [sbt(s2, "tstg%d" % i + n, [128, 1024], BF16) for i in range(2)]
                pcv = [PS[3 * par], PS[3 * par + 1]]; pnm = PS[3 * par + 2]; PSB = PSBs[par]
                npd = 0
                for cb in range(par, 12, 2):
                    S.dma("sp", pc[:, 1:257], s_zg[cb * 128:(cb + 1) * 128, 0:256], pc.name, r=["s_zg_w"], w=[pc])
                    S.dma("sp", pl[:, 1:33, 1:65],
                          s_zg[cb * 128:(cb + 1) * 128, 256:TT].rearrange("p (r c) -> p r c", c=64),
                          pl.name, r=["s_zg_w"], w=[pl])
                    tt("dve", dgb[:], identb[:, :].unsqueeze(1).to_broadcast([128, 9, 128]),
                       cwc[:, cb, :].unsqueeze(2).to_broadcast([128, 9, 128]), ALU.mult)
                    yield
                    isv = cb >= 8
                    dsil = nbb if isv else s32
                    pt = pcv[npd % 2]; npd += 1
                    for dx in range(3):
                        mm(pt[:, 0:256], dgb[:, 3 + dx, :], pc[:, dx:dx + 256], dx == 0, dx == 2)
                    act(dsil[:, 0:256], pt[:, 0:256], AF.Silu)
                    yield
                    for g in range(4):
                        pt = pcv[npd % 2]; npd += 1
                        for tap in range(9):
                            dy, dx = divmod(tap, 3)
                            mm(pt[:, :].rearrange("p (r c) -> p r c", c=64), dgb[:, tap, :],
                               pl[:, 8 * g + dy:8 * g + dy + 8, dx:dx + 64], tap == 0, tap == 8)
                        act(dsil[:, 256 + 512 * g:256 + 512 * (g + 1)], pt[:, :], AF.Silu)
                        yield
                    if cb < 8:
                        scl = (128.0 ** -0.5) if cb < 4 else 1.0
                        for (c0, n_) in [(0, 256)] + [(256 + 512 * g, 512) for g in range(4)]:
                            act(sq[:, 0:n_], s32[:, c0:c0 + n_], AF.Square)
                            mm(pnm[:, 0:n_], ones[:], sq[:, 0:n_])
                            ts("dve", ssb[:, c0:c0 + n_], pnm[:, 0:n_], EPS, ALU.add)
                            yield
                        act(ssb[:], ssb[:], AF.Ln)
                        act(ssb[:], ssb[:], AF.Exp, scale=-0.5)
                        yield
                        for (c0, n_) in [(0, 1280), (1280, 1024)]:
                            stt(nbb[:, c0:c0 + n_], s32[:, c0:c0 + n_], scl, ssb[:, c0:c0 + n_], ALU.mult, ALU.mult)
                            yield
                        dst = s_qTb if cb < 4 else s_kTb
                        S.dma("sp", dst[(cb % 4) * 128:(cb % 4 + 1) * 128, :], nbb[:], nbb.name, r=[nbb],
                              w=[dst.tensor.name + "_w"])
                    if cb >= 4:
                        dstT = s_ktb if cb < 8 else s_vtb
                        for (t0_, nt_) in ((0, 8), (8, 8), (16, 2)):
                            tb = tstg[npd % 2]; npd += 1
                            for j in range(nt_):
                                tr(PSB[:, j * 128:(j + 1) * 128], nbb[:, (t0_ + j) * 128:(t0_ + j + 1) * 128], identb[:])
                            cp("dve", tb[:, 0:nt_ * 128], PSB[:, 0:nt_ * 128])
                            S.dma("sp", dstT[t0_ * 128:(t0_ + nt_) * 128, (cb % 4) * 128:(cb % 4 + 1) * 128].rearrange(
                                "(t p) c -> p t c", p=128), tb[:, 0:nt_ * 128].rearrange("p (t c) -> p t c", c=128),
                                tb.name, r=[tb], w=[dstT.tensor.name + "_w"])
                            yield
            run_chains([d_chain(0), d_chain(1)], [0, 3])
            _barrier(S)
        if debug == 2:
            S.finish()
            return nc

        with ExitStack() as s3:
            oacc = sbt(s3, "oacc", [128, 16, 8, 128])
            with ExitStack() as s4:
                PE_ = [s4.enter_context(nc.psum_tensor("pse%d" % i, [128, 512], F32)) for i in range(8)]

                def mk(name, val, fwd_expr, op, fill, n4=4, dt=F32):
                    tl_ = sbt(s4, name, [128, n4, 128], dt)
                    S.op("pool", lambda e: e.memset(tl_[:], val), w=[tl_])
                    S.op("pool", lambda e: e.affine_select(out=tl_[:], in_=tl_[:], pattern=[[0, n4], [fwd_expr, 128]],
                                                           compare_op=op, fill=fill, base=0,
                                                           channel_multiplier=-fwd_expr), r=[tl_], w=[tl_])
                    return tl_
                triS = [mk("triS0", -1.0 / 16, 1, ALU.is_ge, 0.0, 1), mk("triS1", -1.0 / 16, -1, ALU.is_ge, 0.0, 1)]
                triX = [mk("triX0", -1.0 / 16, -1, ALU.is_gt, 0.0, 1), mk("triX1", -1.0 / 16, 1, ALU.is_gt, 0.0, 1)]
                maskI4 = [mk("maskI0", 1.0, 1, ALU.is_ge, 0.0), mk("maskI1", 1.0, -1, ALU.is_ge, 0.0)]
                posL4 = [mk("posL0", 0.0, -1, ALU.is_gt, 30000.0, dt=BF16), mk("posL1", 0.0, 1, ALU.is_gt, 30000.0, dt=BF16)]
                negE4 = [mk("negE0", 0.0, 1, ALU.is_ge, -30000.0, dt=BF16), mk("negE1", 0.0, -1, ALU.is_ge, -30000.0, dt=BF16)]
                onesb = sbt(s4, "onesb", [128, 128], BF16)
                S.op("pool", lambda e: e.memset(onesb[:], 1.0), w=[onesb])
                upx = sbt(s4, "upx", [33, 2, 256])
                S.op("pool", lambda e: e.memset(upx[:], 0.0), w=[upx])
                for d in range(2):
                    S.dma("sp", upx[d * 16:(d + 1) * 16, d, :], gup_d[d], "upx", w=[upx])
                    S.dma("sp", upx[32:33, d, :], gub_d[d:d + 1, :], "upx", w=[upx])
                prow = sbt(s4, "prow", [1, 16])
                S.dma("sp", prow[0:1, 0:8], alog_d.rearrange("(o n) -> o n", o=1), "prow", w=[prow])
                S.dma("sp", prow[0:1, 8:16], dtb_d.rearrange("(o n) -> o n", o=1), "prow", w=[prow])
                pbc = sbt(s4, "pbc", [128, 16])
                mm(PE_[0][:, 0:16], ones[0:1, :], prow[0:1, :])
                cp("dve", pbc[:], PE_[0][:, 0:16])
                act(pbc[:, 0:8], pbc[:, 0:8], AF.Exp)
                ts("dve", pbc[:, 0:8], pbc[:, 0:8], -1.0, ALU.mult)
                bdall = sbt(s4, "bdall", [128, NTILE, 16])
                S.dma("sp", bdall[:], s_bd.rearrange("(t p) c -> p t c", p=128), "bdall", r=["s_bd_w"], w=[bdall])
                LNBall = sbt(s4, "LNBall", [128, NTILE, 8]); BETAall = sbt(s4, "BETAall", [128, NTILE, 8])
                GGall = sbt(s4, "GGall", [128, NTILE, 8])
                act(LNBall[:], bdall[:, :, 0:8], AF.Exp, scale=-1.0)
                act(LNBall[:], LNBall[:], AF.Ln, bias=1.0)
                act(BETAall[:], LNBall[:], AF.Exp, scale=-1.0)
                ts("dve", LNBall[:], LNBall[:], -1.0, ALU.mult)
                tt("dve", GGall[:], bdall[:, :, 8:16], pbc[:, 8:16].unsqueeze(1).to_broadcast([128, NTILE, 8]), ALU.add)
                act(GGall[:], GGall[:], AF.Exp)
                act(GGall[:], GGall[:], AF.Ln, bias=1.0)
                tt("dve", GGall[:], GGall[:], pbc[:, 0:8].unsqueeze(1).to_broadcast([128, NTILE, 8]), ALU.mult)
                def gcol(nm, dt=F32):
                    return sbt(s4, nm, [128, NTILE, 8], dt)
                GAMall = gcol("GAMall"); GLall = gcol("GLall"); CLall = gcol("CLall"); NGAMall = gcol("NGAMall")
                EGAMall = gcol("EGAMall"); BEKall = gcol("BEKall"); EKDall = gcol("EKDall"); EDECall = gcol("EDECall")
                GB16all = gcol("GB16all", BF16); GLOall = gcol("GLOall")
                for t_ in range(NTILE):
                    for d_ in range(2):
                        c_ = t_ * 8 + d_ * 4
                        mm(PE_[0][:, c_:c_ + 4], maskI4[d_][:, 0, :], GGall[:, t_, d_ * 4:(d_ + 1) * 4])
                        mm(PE_[1][:, c_:c_ + 4], ones[:], GGall[:, t_, d_ * 4:(d_ + 1) * 4])
                fl = lambda a: a[:].rearrange("p a b -> p (a b)")
                cp("dve", fl(GAMall), PE_[0][:, 0:NTILE * 8])
                cp("act", fl(GLall), PE_[1][:, 0:NTILE * 8])
                tt("dve", CLall[:], LNBall[:], GAMall[:], ALU.add)
                ts("dve", NGAMall[:], GAMall[:], -1.0, ALU.mult)
                tt("dve", EKDall[:], GLall[:], GAMall[:], ALU.subtract)
                act(EGAMall[:], GAMall[:], AF.Exp)
                act(EKDall[:], EKDall[:], AF.Exp)
                act(EDECall[:], GLall[:], AF.Exp)
                tt("dve", BEKall[:], BETAall[:], EGAMall[:], ALU.mult)
                cp("dve", GB16all[:], GGall[:])
                tt("dve", GLOall[:], GGall[:], GB16all[:], ALU.subtract)
                hm = sbt(s4, "hm", [128, 2])
                S.op("pool", lambda e: e.memset(hm[:], 0.0), w=[hm])
                S.op("pool", lambda e: e.memset(hm[0:64, 0:1], 1.0), r=[hm], w=[hm])
                S.op("pool", lambda e: e.memset(hm[64:128, 1:2], 1.0), r=[hm], w=[hm])
                owritten = set()

                def owrite(t, h0, pbank):
                    ov = oacc[:, t - 2, h0:h0 + 4, :]
                    pv = pbank[:, :].rearrange("p (a b) -> p a b", b=128)
                    if (t, h0) not in owritten:
                        owritten.add((t, h0))
                        cp("dve", ov, pv)
                    else:
                        tt("dve", ov, pv, ov, ALU.add)

                def orders(d):
                    return list(range(NTILE)) if d == 0 else [1, 0] + list(range(NTILE - 1, 1, -1))

                def v3(a):
                    return a[:, :].rearrange("p (a b) -> p a b", b=128)

                def gla_chain(d):
                    n = "a%d" % d
                    pg = PE_[d]
                    def db(nm, shape, dt):
                        return [sbt(s4, nm + n + str(i), shape, dt) for i in range(2)]
                    qTa = db("qTa", [128, 2, 128], BF16); kTa = db("kTa", [128, 2, 128], BF16)
                    kta = db("kta", [128, 256], BF16); vta = db("vta", [128, 512], BF16); alrx = db("alrx", [33, 128], F32)
                    for i in range(2):
                        S.op("pool", lambda e, i=i: e.memset(alrx[i][:], 1.0), w=[alrx[i]])
                    l1 = sbt(s4, "l1" + n, [128, 256]); ebT = sbt(s4, "ebT" + n, [128, 256]); enb = sbt(s4, "enb" + n, [128, 256])
                    qg = sbt(s4, "qg" + n, [128, 2, 128], BF16); kg = sbt(s4, "kg" + n, [128, 2, 128], BF16)
                    qgm = sbt(s4, "qgm" + n, [128, 4, 128], BF16)
                    ekd = sbt(s4, "ekd" + n, [128, 256]); kd = sbt(s4, "kd" + n, [128, 256], BF16)
                    attm = sbt(s4, "attm" + n, [128, 512], BF16)
                    Sa32 = sbt(s4, "Sa32" + n, [128, 2, 128]); Sa16 = sbt(s4, "Sa16" + n, [128, 2, 128], BF16)
                    S.op("pool", lambda e: e.memset(Sa32[:], 0.0), w=[Sa32])
                    S.op("pool", lambda e: e.memset(Sa16[:], 0.0), w=[Sa16])
                    order = orders(d)
                    last = 127 if d == 0 else 0

                    def load(t, bi):
                        tk = slice(t * 128, (t + 1) * 128)
                        S.dma("sp", qTa[bi][:], s_qTa[:, tk].rearrange("(h p) t -> p h t", p=128), qTa[bi].name, r=["s_qTa_w"], w=[qTa[bi]])
                        S.dma("sp", kTa[bi][:], s_kTa[:, tk].rearrange("(h p) t -> p h t", p=128), kTa[bi].name, r=["s_kTa_w"], w=[kTa[bi]])
                        S.dma("sp", kta[bi][:], s_kta[tk, :], kta[bi].name, r=["s_kta_w"], w=[kta[bi]])
                        S.dma("sp", vta[bi][:], s_vta[tk, :], vta[bi].name, r=["s_vta_w"], w=[vta[bi]])
                        S.dma("sp", alrx[bi][0:32, :], s_alr[:, tk], alrx[bi].name, r=["s_alr_w"], w=[alrx[bi]])
                    load(order[0], 0)
                    for si, t in enumerate(order):
                        bi = si % 2
                        if si + 1 < len(order):
                            load(order[si + 1], (si + 1) % 2)
                        lat = t >= 2
                        q_a, k_a, kt_a, vt_a, al_ = qTa[bi], kTa[bi], kta[bi], vta[bi], alrx[bi]
                        mm(pg[:, 0:256], al_[0:33, :], upx[0:33, d, :])
                        act(l1[:], pg[:, 0:256], AF.Exp, scale=-1.0)
                        act(l1[:], l1[:], AF.Ln, bias=1.0)
                        yield
                        for hp in range(2):
                            mm(pg[:, 256 + hp * 128:256 + (hp + 1) * 128], l1[:, hp * 128:(hp + 1) * 128], triS[d][:, 0, :])
                        act(ebT[:], pg[:, 256:512], AF.Exp)
                        act(enb[:], pg[:, 256:512], AF.Exp, scale=-1.0)
                        yield
                        mm(pg[:, 0:256], triX[d][:, 0, :], l1[:])
                        stt(qg[:].rearrange("p a b -> p (a b)"), q_a[:].rearrange("p a b -> p (a b)"), 0.125, ebT[:],
                            ALU.mult, ALU.mult)
                        tt("dve", kg[:].rearrange("p a b -> p (a b)"), k_a[:].rearrange("p a b -> p (a b)"), enb[:], ALU.mult)
                        act(ekd[:], pg[:, 0:256], AF.Exp)
                        tt("dve", kd[:], kt_a[:], ekd[:], ALU.mult)
                        yield
                        if lat:
                            for hp in range(2):
                                tt("pool", qgm[:, 2 * hp:2 * hp + 2, :], qg[:, hp:hp + 1, :].to_broadcast([128, 2, 128]),
                                   hm[:, :].unsqueeze(2).to_broadcast([128, 2, 128]), ALU.mult)
                            for h in range(4):
                                mm(pg[:, h * 128:(h + 1) * 128], kg[:, h // 2, :], qgm[:, h, :])
                            tt("dve", attm[:], pg[:, :], maskI4[d][:].rearrange("p a b -> p (a b)"), ALU.mult)
                            yield
                            for h in range(4):
                                mm(pg[:, h * 128:(h + 1) * 128], vt_a[:, h * 128:(h + 1) * 128],
                                   attm[:, h * 128:(h + 1) * 128], True, False)
                                mm(pg[:, h * 128:(h + 1) * 128], Sa16[:, h // 2, :], qgm[:, h, :], False, True)
                            owrite(t, 0, pg)
                            yield
                        for hp in range(2):
                            mm(pg[:, 0:256], kd[:, hp * 128:(hp + 1) * 128], vt_a[:, hp * 256:(hp + 1) * 256])
                            for hf in range(2):
                                rsl = slice(hf * 64, (hf + 1) * 64)
                                stt(Sa32[rsl, hp, :], Sa32[rsl, hp, :], ebT[rsl, hp * 128 + last:hp * 128 + last + 1],
                                    pg[rsl, hf * 128:(hf + 1) * 128], ALU.mult, ALU.add)
                        cp("act", Sa16[:], Sa32[:])
                        yield

                def gdn_chain(d):
                    n = "b%d" % d
                    pa, pb, pc = PE_[2 + 3 * d], PE_[3 + 3 * d], PE_[4 + 3 * d]
                    def db(nm, shape, dt):
                        return [sbt(s4, nm + n + str(i), shape, dt) for i in range(2)]
                    qTb = db("qTb", [128, 4, 128], BF16); kTb = db("kTb", [128, 4, 128], BF16)
                    ktb = db("ktb", [128, 512], BF16); vtb = db("vtb", [128, 512], BF16); bdt = db("bdt", [128, 16], F32)
                    cols = sbt(s4, "cols" + n, [128, 16, 4])
                    f32t = lambda nm: sbt(s4, nm + n, [128, 512])
                    b16t = lambda nm: sbt(s4, nm + n, [128, 512], BF16)
                    rGh = b16t("rGh"); rGl = b16t("rGl"); XA = f32t("XA"); XB = f32t("XB"); Lm = f32t("Lm"); LTm = f32t("LTm")
                    Mx = [f32t("Mx0"), f32t("Mx1")]; Mtx = [f32t("Mtx0"), f32t("Mtx1")]; Pt = f32t("Pt"); u32 = f32t("u32")
                    Tt16 = b16t("Tt16"); vb = b16t("vb"); kbg = b16t("kbg"); kdb = b16t("kdb"); wT16 = b16t("wT16")
                    aqk = b16t("aqk"); qgb = b16t("qgb"); vnew = b16t("vnew")
                    Sb32 = sbt(s4, "Sb32" + n, [128, 4, 128]); Sb16 = sbt(s4, "Sb16" + n, [128, 4, 128], BF16)
                    S.op("pool", lambda e: e.memset(Sb32[:], 0.0), w=[Sb32])
                    S.op("pool", lambda e: e.memset(Sb16[:], 0.0), w=[Sb16])
                    BETA, LNB, YY, GG, GAM, GL, CL, NGAM, EGAM, BEK, EKD, EDEC, GLO = range(13)
                    colv = {}
                    C = lambda i: colv.get(i, cols[:, i, :])
                    CB = lambda i: C(i).unsqueeze(2).to_broadcast([128, 4, 128])
                    order = orders(d)

                    def load(t, bi):
                        tk = slice(t * 128, (t + 1) * 128)
                        S.dma("sp", qTb[bi][:], s_qTb[:, tk].rearrange("(h p) t -> p h t", p=128), qTb[bi].name, r=["s_qTb_w"], w=[qTb[bi]])
                        S.dma("sp", kTb[bi][:], s_kTb[:, tk].rearrange("(h p) t -> p h t", p=128), kTb[bi].name, r=["s_kTb_w"], w=[kTb[bi]])
                        S.dma("sp", ktb[bi][:], s_ktb[tk, :], ktb[bi].name, r=["s_ktb_w"], w=[ktb[bi]])
                        S.dma("sp", vtb[bi][:], s_vtb[tk, :], vtb[bi].name, r=["s_vtb_w"], w=[vtb[bi]])
                    load(order[0], 0)
                    for si, t in enumerate(order):
                        bi = si % 2
                        if si + 1 < len(order):
                            load(order[si + 1], (si + 1) % 2)
                        lat = t >= 2
                        q_b, k_b, kt_b, vt_b, bd_ = qTb[bi], kTb[bi], ktb[bi], vtb[bi], bdt[bi]
                        gsl = slice(d * 4, (d + 1) * 4)
                        colv = {BETA: BETAall[:, t, gsl], LNB: LNBall[:, t, gsl], GG: GGall[:, t, gsl],
                                GAM: GAMall[:, t, gsl], GL: GLall[:, t, gsl], CL: CLall[:, t, gsl],
                                NGAM: NGAMall[:, t, gsl], EGAM: EGAMall[:, t, gsl], BEK: BEKall[:, t, gsl],
                                EKD: EKDall[:, t, gsl], EDEC: EDECall[:, t, gsl], GLO: GLOall[:, t, gsl]}
                        gb16 = GB16all[:, t, gsl]
                        tt("pool", v3(rGh), maskI4[d][:], gb16.unsqueeze(2).to_broadcast([128, 4, 128]), ALU.mult)
                        tt("pool", v3(rGl), maskI4[d][:], CB(GLO), ALU.mult)
                        tt("pool", Sb32[:], Sb32[:], CB(EDEC), ALU.mult)
                        yield
                        mm(pa[:, :], onesb[:], rGh[:], True, False)
                        mm(pa[:, :], onesb[:], rGl[:], False, False)
                        mm(pa[:, :], identb[:], posL4[d][:].rearrange("p a b -> p (a b)"), False, True)
                        for h in range(4):
                            mm(pb[:, h * 128:(h + 1) * 128], k_b[:, h, :], k_b[:, h, :])
                        for h in range(4):
                            act(XA[:, h * 128:(h + 1) * 128], pa[:, h * 128:(h + 1) * 128], AF.Exp,
                                bias=C(CL)[:, h:h + 1], scale=-1.0)
                        yield
                        tt("dve", Lm[:], pb[:, :], XA[:], ALU.mult)
                        tt("pool", v3(vb), v3(vt_b), CB(BETA), ALU.mult)
                        yield
                        for h in range(4):
                            tr(pb[:, h * 128:(h + 1) * 128], Lm[:, h * 128:(h + 1) * 128], ident[:])
                        if lat:
                            mm(pa[:, :], onesb[:], rGh[:], True, False)
                            mm(pa[:, :], onesb[:], rGl[:], False, False)
                            mm(pa[:, :], identb[:], negE4[d][:].rearrange("p a b -> p (a b)"), False, True)
                            for h in range(4):
                                mm(pc[:, h * 128:(h + 1) * 128], k_b[:, h, :], q_b[:, h, :])
                        cp("act", LTm[:], pb[:, :])
                        yield
                        tt("dve", v3(Pt), ident[:, :].unsqueeze(1).to_broadcast([128, 4, 128]), v3(pb), ALU.subtract)
                        if lat:
                            for h in range(4):
                                act(XB[:, h * 128:(h + 1) * 128], pa[:, h * 128:(h + 1) * 128], AF.Exp,
                                    bias=C(NGAM)[:, h:h + 1])
                            yield
                            tt("dve", aqk[:], pc[:, :], XB[:], ALU.mult)
                            mm(pa[:, :], onesb[:], rGh[:], True, False)
                            mm(pa[:, :], onesb[:], rGl[:], False, True)
                            act(XA[:], pa[:, :], AF.Exp)
                            yield
                            tt("pool", qgb[:], q_b[:].rearrange("p a b -> p (a b)"), XA[:], ALU.mult)
                        Mc, Mtc = Lm, LTm
                        for lev in range(6):
                            Mn, Mtn = Mx[lev % 2], Mtx[lev % 2]
                            for h in range(4):
                                hs = slice(h * 128, (h + 1) * 128)
                                mm(pa[:, hs], Mtc[:, hs], Mc[:, hs])
                            if lev < 5:
                                for h in range(4):
                                    hs = slice(h * 128, (h + 1) * 128)
                                    mm(pc[:, hs], Mc[:, hs], Mtc[:, hs])
                            cp("act", Mn[:], pa[:, :])
                            yield
                            if lev < 5:
                                cp("act", Mtn[:], pc[:, :])
                            for h in range(4):
                                hs = slice(h * 128, (h + 1) * 128)
                                mm(pb[:, hs], Mn[:, hs], Pt[:, hs])
                            yield
                            if lev < 5:
                                tt("dve", Pt[:], pb[:, :], Pt[:], ALU.add)
                            else:
                                tt("dve", Tt16[:], pb[:, :], Pt[:], ALU.add)
                            if lev == 2:
                                tt("pool", v3(kbg), v3(kt_b), CB(BEK), ALU.mult)
                            if lev == 3:
                                tt("pool", v3(kdb), v3(kt_b), CB(EKD), ALU.mult)
                            yield
                            Mc, Mtc = Mn, Mtn
                        for h in range(4):
                            hs = slice(h * 128, (h + 1) * 128)
                            mm(pa[:, hs], Tt16[:, hs], vb[:, hs])
                            mm(pc[:, hs], kbg[:, hs], Tt16[:, hs])
                        cp("act", u32[:], pa[:, :])
                        cp("dve", wT16[:], pc[:, :])
                        yield
                        for h in range(4):
                            hs = slice(h * 128, (h + 1) * 128)
                            mm(pb[:, hs], wT16[:, hs], Sb16[:, h, :])
                        tt("dve", vnew[:], u32[:], pb[:, :], ALU.subtract)
                        yield
                        if lat:
                            for h in range(4):
                                hs = slice(h * 128, (h + 1) * 128)
                                mm(pa[:, hs], Sb16[:, h, :], qgb[:, hs], True, False)
                                mm(pa[:, hs], vnew[:, hs], aqk[:, hs], False, True)
                        for h in range(4):
                            hs = slice(h * 128, (h + 1) * 128)
                            mm(pc[:, hs], kdb[:, hs], vnew[:, hs])
                        if lat:
                            owrite(t, 4, pa)
                        yield
                        tt("dve", Sb16[:].rearrange("p a b -> p (a b)"), pc[:, :], Sb32[:].rearrange("p a b -> p (a b)"), ALU.add)
                        tt("dve", Sb32[:].rearrange("p a b -> p (a b)"), pc[:, :], Sb32[:].rearrange("p a b -> p (a b)"), ALU.add)
                        yield

                run_chains([gdn_chain(0), gdn_chain(1), gla_chain(0), gla_chain(1)], [0, 12, 0, 8], [1, 1, 3, 3])
                _barrier(S)
            if debug == 3:
                S.dma("sp", s_oacc[:, :], oacc[:].rearrange("p a b c -> p (a b c)"), "dbg", r=[oacc])
                S.finish()
                return nc

            with ExitStack() as s5:
                PS = [s5.enter_context(nc.psum_tensor("psf%d" % i, [128, 512], F32)) for i in range(8)]
                wga = sbt(s5, "wga", [128, 4, 1024], BF16); wgd = sbt(s5, "wgd", [128, 4, 1024], BF16)
                wo = sbt(s5, "wo", [128, 8, 1024], BF16)
                S.dma("pool", wga[:], wga_d.rearrange("(k p) n -> p k n", p=128), "wga", w=[wga])
                S.dma("pool", wgd[:], wgd_d.rearrange("(k p) n -> p k n", p=128), "wgd", w=[wgd])
                S.dma("pool", wo[:], wo_d.rearrange("(k p) n -> p k n", p=128), "wo", w=[wo])
                onc = sbt(s5, "onc", [128, 2])
                S.dma("sp", onc[:, 0:1], gon_d.rearrange("(p o) -> p o", o=1), "onc", w=[onc])
                S.dma("sp", onc[:, 1:2], don_d.rearrange("(p o) -> p o", o=1), "onc", w=[onc])
                om = sbt(s5, "om", [128, 128])
                S.op("pool", lambda e: e.memset(om[:], 1.0 / 128), w=[om])
                gmb = [sbt(s5, "gmb%d" % i, [128, 16, 512], BF16) for i in range(2)]
                yab = [sbt(s5, "yab%d" % i, [128, 8, 512], BF16) for i in range(2)]
                t1 = [sbt(s5, "t1_%d" % i, [128, 512]) for i in range(2)]
                t2 = [sbt(s5, "t2_%d" % i, [128, 512]) for i in range(2)]
                mrg = sbt(s5, "mrg", [128, 8, 512], BF16)
                xr = [sbt(s5, "xr%d" % i, [128, 1024]) for i in range(2)]
                xw = [sbt(s5, "xw%d" % i, [128, 1024]) for i in range(2)]
                jk = sbt(s5, "jk", [128, 1024], BF16)
                fs = sbt(s5, "fs", [128, 4])
                v3 = lambda a: a[:, :].rearrange("p (a b) -> p a b", b=128)

                fprog = {"b": -1, "a0": -1, "a1": -1}

                def f_norm_chain(par):
                    n = "n%d" % par
                    gab = [sbt(s5, "gab0" + n, [128, 512], BF16)] * 2
                    sqf = sbt(s5, "sqf" + n, [128, 512]); rsf = sbt(s5, "rsf" + n, [128, 512])
                    pt = PS[par]
                    k = 0
                    for g in range(4):
                        while fprog["b"] < g - 2:
                            yield
                        tks = slice(g * 512, (g + 1) * 512)
                        if par == 0:
                            S.dma("sp", gmb[g % 2][:], s_gm[:, tks].rearrange("(m p) t -> p m t", p=128),
                                  gmb[g % 2].name, r=["s_gm_w"], w=[gmb[g % 2]])
                        for hd in range(par, 8, 2):
                            gb_ = gab[k % 2]; k += 1
                            src = s_ga if hd < 4 else s_gb
                            S.dma("sp", gb_[:], src[(hd % 4) * 128:(hd % 4 + 1) * 128, tks], gb_.name,
                                  r=[src.tensor.name + "_w"], w=[gb_])
                            ov = oacc[:, 4 * g:4 * g + 4, hd, :]
                            tt("pool", v3(sqf), ov, ov, ALU.mult)
                            mm(pt[:, :], om[:], sqf[:])
                            yield
                            rsqrt_(rsf[:], pt[:, :], EPS)
                            yield
                            tt("pool", v3(sqf), ov, v3(rsf), ALU.mult)
                            yield
                            stt(yab[g % 2][:, hd, :], sqf[:], onc[:, (0 if hd < 4 else 1):(1 if hd < 4 else 2)], gb_[:],
                                ALU.mult, ALU.mult)
                            yield
                        fprog["a%d" % par] = g

                mrg2 = [mrg, sbt(s5, "mrgb", [128, 8, 512], BF16)]
                fprog["p"] = -1; fprog["o"] = -1

                def f_proj_chain():
                    nf = 0
                    for g in range(4):
                        while min(fprog["a0"], fprog["a1"]) < g or fprog["o"] < g - 2:
                            yield
                        ya_, gm_, mg_ = yab[g % 2], gmb[g % 2], mrg2[g % 2]
                        for nb_ in range(8):
                            pa, pb = PS[2 + nf % 2], PS[4 + nf % 2]
                            t1_, t2_ = t1[nf % 2], t2[nf % 2]; nf += 1
                            for kc in range(4):
                                mm(pa[:, :], wga[:, kc, nb_ * 128:(nb_ + 1) * 128], ya_[:, kc, :], kc == 0, kc == 3)
                            for kc in range(4):
                                mm(pb[:, :], wgd[:, kc, nb_ * 128:(nb_ + 1) * 128], ya_[:, 4 + kc, :], kc == 0, kc == 3)
                            yield
                            tt("dve", t1_[:], pa[:, :], gm_[:, nb_, :], ALU.mult)
                            tt("dve", t2_[:], pb[:, :], gm_[:, 8 + nb_, :], ALU.mult)
                            yield
                            tt("pool", mg_[:, nb_, :], t1_[:], t2_[:], ALU.add)
                            yield
                        fprog["b"] = g
                        fprog["p"] = g

                def f_out_chain():
                    for g in range(4):
                        while fprog["p"] < g:
                            yield
                        mg_ = mrg2[g % 2]
                        for tl in range(4):
                            tile = 4 * g + tl
                            xr_, xw_ = xr[tile % 2], xw[tile % 2]
                            S.dma("sp", xr_[:], x_d[tile * 128:(tile + 1) * 128, :], xr_.name, w=[xr_])
                            for c2 in range(2):
                                pt = PS[6 + c2]
                                cs = slice(c2 * 512, (c2 + 1) * 512)
                                for kc in range(8):
                                    mm(pt[:, :], mg_[:, kc, tl * 128:(tl + 1) * 128], wo[:, kc, cs], kc == 0, kc == 7)
                                tt("dve", xw_[:, cs], pt[:, :], gtbc[:, cs], ALU.mult)
                                yield
                            tt("pool", xw_[:], xw_[:], xr_[:], ALU.add)
                            yield
                            S.op("act", lambda e, xw_=xw_: e.activation(out=jk[:], in_=xw_[:], func=AF.Square,
                                                                       accum_out=fs[:, 0:1]), r=[xw_], w=[jk, fs])
                            ts("dve", fs[:, 1:2], fs[:, 0:1], 1.0 / D, ALU.mult)
                            rsqrt_(fs[:, 2:3], fs[:, 1:2], EPS)
                            yield
                            stt(xr_[:], xw_[:], fs[:, 2:3], fgbc[:], ALU.mult, ALU.mult)
                            S.dma("sp", out_d[tile * 128:(tile + 1) * 128, :], xr_[:], xr_.name, r=[xr_], w=["out_w"])
                            yield
                        fprog["o"] = g
                run_chains([f_norm_chain(0), f_norm_chain(1), f_proj_chain(), f_out_chain()], [0, 2, 24, 24])
        S.finish()
    return nc


_NC = {}


def kernel(**inputs):
    nc = _NC.get("nc")
    if nc is None:
        nc = _NC["nc"] = build_program()
    in_maps = make_in_maps(inputs)
    res = run_bass_kernel_spmd(nc, in_maps, core_ids=list(range(8)))
    return np.stack([r["out"] for r in res.results], axis=0).astype(np.float32)


def make_in_maps(inputs):
    f = lambda a: np.ascontiguousarray(np.asarray(a, dtype=np.float32))
    maps = []
    for b in range(8):
        maps.append({
            "x": f(inputs["x"][b]), "c": f(inputs["c"][b]), "ctx": f(inputs["ctx"][b]), "c_ctx": f(inputs["c_ctx"]),
            "w_mod": f(inputs["w_mod"][0]), "b_mod": f(inputs["b_mod"][0]), "norm_g": f(inputs["norm_g"][0]),
            "w_in": f(inputs["w_in"][0]), "gla_up": f(inputs["gla_up"][0]), "gla_ub": f(inputs["gla_ub"][0]),
            "gla_onorm": f(inputs["gla_onorm"][0]), "gdn_conv": f(inputs["gdn_conv"][0]).reshape(9, 1536),
            "gdn_a_log": f(inputs["gdn_a_log"][0]).reshape(8), "gdn_dt_bias": f(inputs["gdn_dt_bias"][0]).reshape(8),
            "gdn_onorm": f(inputs["gdn_onorm"][0]), "w_gla_out": f(inputs["w_gla_out"][0]),
            "w_gdn_out": f(inputs["w_gdn_out"][0]), "b_gate": f(inputs["b_gate"][0]), "w_o": f(inputs["w_o"][0]),
            "final_g": f(inputs["final_g"]),
        })
    return maps
```

```python
import numpy as np
from contextlib import ExitStack
import concourse.bass as bass
import concourse.mybir as mybir
from concourse.bass_utils import run_bass_kernel_spmd

F32 = mybir.dt.float32
BF16 = mybir.dt.bfloat16
AF = mybir.ActivationFunctionType
ALU = mybir.AluOpType
AX = mybir.AxisListType

D = 1024
T = 2048
TC = 256
TT = T + TC
NTILE = TT // 128
DIN = 5680
EPS = 1e-6
NEG = -30000.0

C_AQ, C_AK, C_AV, C_AG, C_ALR = 0, 256, 512, 1024, 1536
C_BQ, C_BK, C_BV, C_BG, C_BB, C_BD, C_MG = 1568, 2080, 2592, 3104, 3616, 3624, 3632


class Sched:
    ENGS = ("pe", "act", "dve", "pool", "sp")

    def __init__(self, nc, st):
        self.nc = nc
        self.st = st
        self.eng = {"pe": nc.tensor, "act": nc.scalar, "dve": nc.vector, "pool": nc.gpsimd, "sp": nc.sync}
        self.esem = {e: st.enter_context(nc.semaphore("es_" + e)) for e in self.ENGS}
        self.cnt = {e: 0 for e in self.ENGS}
        self.waited = {e: {} for e in self.ENGS}
        self.lastw = {}
        self.readers = {}
        self.dsem = {}
        self.sems = {}
        for e in self.ENGS:
            self.sems[self.esem[e].name] = self.esem[e]
        self.ninst = 0

    def _waits(self, e, evs):
        for (sn, v) in evs:
            if self.waited[e].get(sn, 0) >= v:
                continue
            self.waited[e][sn] = v
            self.eng[e].wait_ge(self.sems[sn], v)

    def _deps(self, e, rk, wk):
        evs = []
        own = self.esem[e].name
        for k in rk:
            ev = self.lastw.get(k)
            if ev is not None:
                evs.append(ev)
        for k in wk:
            ev = self.lastw.get(k)
            if ev is not None:
                evs.append(ev)
            for ev in self.readers.get(k, ()):
                evs.append(ev)
        if e == "pe":
            evs = [ev for ev in evs if ev[0] != own]
        return evs

    def _commit(self, ev, rk, wk):
        for k in rk:
            lst = self.readers.setdefault(k, [])
            lst[:] = [x for x in lst if x[0] != ev[0]] + [ev]
        for k in wk:
            self.lastw[k] = ev
            self.readers[k] = []

    @staticmethod
    def keys(aps):
        out = []
        for a in aps:
            if a is None:
                continue
            if isinstance(a, str):
                out.append(a)
            elif hasattr(a, "tensor"):
                out.append(a.tensor.name)
            else:
                out.append(a.name)
        return out

    def op(self, e, fn, r=(), w=()):
        rk, wk = self.keys(r), self.keys(w)
        wk = wk + [k for k in rk if k.startswith("ps") and k not in wk]
        self._waits(e, self._deps(e, rk, wk))
        ins = fn(self.eng[e])
        ins.then_inc(self.esem[e], 1)
        self.cnt[e] += 1
        self.ninst += 1
        self._commit((self.esem[e].name, self.cnt[e]), rk, wk)

    def dma(self, e, out, in_, semkey, r=(), w=(), **kw):
        rk, wk = self.keys(r), self.keys(w)
        self._waits(e, self._deps(e, rk, wk))
        if semkey not in self.dsem:
            s = self.st.enter_context(self.nc.semaphore("ds_%d" % len(self.dsem)))
            self.dsem[semkey] = [s, 0]
            self.sems[s.name] = s
        s = self.dsem[semkey]
        self.eng[e].dma_start(out=out, in_=in_, **kw).then_inc(s[0], 16)
        s[1] += 16
        self.ninst += 1
        self._commit((s[0].name, s[1]), rk, wk)

    def finish(self, e="sp"):
        for k, (s, v) in self.dsem.items():
            self.eng[e].wait_ge(s, v)
        for e2 in self.ENGS:
            if e2 != e and self.cnt[e2] > 0:
                self.eng[e].wait_ge(self.esem[e2], self.cnt[e2])


def _barrier(S):
    tgt = [(S.esem[e].name, S.cnt[e]) for e in S.ENGS if S.cnt[e] > 0]
    tgt += [(s.name, v) for (s, v) in S.dsem.values()]
    for e in S.ENGS:
        S._waits(e, [ev for ev in tgt if ev[0] != S.esem[e].name])


def run_chains(chains, delays=None, strides=None):
    alive = list(chains)
    delays = dict(zip(map(id, chains), delays or [0] * len(chains)))
    strides = dict(zip(map(id, chains), strides or [1] * len(chains)))
    rnd = 0
    while alive:
        rnd += 1
        for g_ in list(alive):
            if delays[id(g_)] > 0:
                delays[id(g_)] -= 1
                continue
            if rnd % strides[id(g_)] != 0 and any(strides[id(o_)] == 1 for o_ in alive):
                continue
            try:
                next(g_)
            except StopIteration:
                alive.remove(g_)


def build_program(debug=0):
    nc = bass.Bass("TRN2", target_bir_lowering=False)
    dk = "ExternalOutput" if debug else "Internal"

    def din(name, shape):
        return nc.dram_tensor(name, list(shape), F32, kind="ExternalInput").ap()

    x_d = din("x", [T, D]); c_d = din("c", [D]); ctx_d = din("ctx", [TC, D]); cctx_d = din("c_ctx", [D])
    wmod_d = din("w_mod", [D, 3 * D]); bmod_d = din("b_mod", [3 * D]); ng_d = din("norm_g", [D])
    win_d = din("w_in", [D, DIN]); gup_d = din("gla_up", [2, 16, 256]); gub_d = din("gla_ub", [2, 256])
    gon_d = din("gla_onorm", [128]); conv_d = din("gdn_conv", [9, 1536]); alog_d = din("gdn_a_log", [8])
    dtb_d = din("gdn_dt_bias", [8]); don_d = din("gdn_onorm", [128]); wga_d = din("w_gla_out", [512, D])
    wgd_d = din("w_gdn_out", [512, D]); bg_d = din("b_gate", [2 * D]); wo_d = din("w_o", [D, D])
    fg_d = din("final_g", [D])
    out_d = nc.dram_tensor("out", [T, D], F32, kind="ExternalOutput").ap()

    def scr(name, shape, dt):
        return nc.dram_tensor(name, list(shape), dt, kind=dk).ap()

    s_qTa = scr("s_qTa", [256, TT], BF16); s_kTa = scr("s_kTa", [256, TT], BF16)
    s_kta = scr("s_kta", [TT, 256], BF16); s_vta = scr("s_vta", [TT, 512], BF16)
    s_alr = scr("s_alr", [32, TT], F32); s_zg = scr("s_zg", [1536, TT], BF16)
    s_bd = scr("s_bd", [TT, 16], F32)
    s_ga = scr("s_ga", [512, T], BF16); s_gb = scr("s_gb", [512, T], BF16); s_gm = scr("s_gm", [2048, T], BF16)
    s_qTb = scr("s_qTb", [512, TT], BF16); s_kTb = scr("s_kTb", [512, TT], BF16)
    s_ktb = scr("s_ktb", [TT, 512], BF16); s_vtb = scr("s_vtb", [TT, 512], BF16)
    if debug:
        s_oacc = scr("s_oacc", [128, 16 * 8 * 128], F32)

    with ExitStack() as st:
        S = Sched(nc, st)

        def sbt(stk, name, shape, dt=F32):
            return stk.enter_context(nc.sbuf_tensor(name, list(shape), dt))

        def sb(name, shape, dt=F32):
            return sbt(st, name, shape, dt)

        ident = sb("ident", [128, 128]); ones = sb("ones", [128, 128]); identb = sb("identb", [128, 128], BF16)
        S.op("pool", lambda e: e.memset(ones[:], 1.0), w=[ones])
        S.op("pool", lambda e: e.memset(ident[:], 1.0), w=[ident])
        S.op("pool", lambda e: e.affine_select(out=ident[:], in_=ident[:], pattern=[[1, 128]], compare_op=ALU.is_equal,
                                               fill=0.0, base=0, channel_multiplier=-1), r=[ident], w=[ident])
        S.op("pool", lambda e: e.tensor_copy(out=identb[:], in_=ident[:]), r=[ident], w=[identb])


        modc = sb("modc", [128, 16, 2]); Acol = sb("Acol", [128, 8, 2])
        gtbc = sb("gtbc", [128, 1024]); fgbc = sb("fgbc", [128, 1024])

        with ExitStack() as s1:
            PS = [s1.enter_context(nc.psum_tensor("psa%d" % i, [128, 512], F32)) for i in range(7)]
            win = sbt(s1, "win", [128, 8, DIN], BF16)
            for kc in range(8):
                S.dma("pool", win[:, kc, :], win_d[kc * 128:(kc + 1) * 128, :], "win", w=[win])
            cT = sbt(s1, "cT", [128, 8, 2])
            S.dma("sp", cT[:, :, 0], c_d.rearrange("(k p) -> p k", p=128), "cT", w=[cT], allow_slow_non_contiguous=True)
            S.dma("sp", cT[:, :, 1], cctx_d.rearrange("(k p) -> p k", p=128), "cT", w=[cT],
                  allow_slow_non_contiguous=True)
            scT = sbt(s1, "scT", [128, 8, 2])
            S.op("act", lambda e: e.activation(out=scT[:], in_=cT[:], func=AF.Silu), r=[cT], w=[scT])
            bmodc = sbt(s1, "bmodc", [128, 16])
            S.dma("sp", bmodc[:], bmod_d[0:2048].rearrange("(m p) -> p m", p=128), "bmodc", w=[bmodc],
                  allow_slow_non_contiguous=True)
            ngc = sbt(s1, "ngc", [128, 8])
            S.dma("sp", ngc[:], ng_d.rearrange("(k p) -> p k", p=128), "ngc", w=[ngc], allow_slow_non_contiguous=True)
            bgc = sbt(s1, "bgc", [128, 16])
            S.dma("sp", bgc[:], bg_d.rearrange("(m p) -> p m", p=128), "bgc", w=[bgc], allow_slow_non_contiguous=True)
            rowbuf = sbt(s1, "rowbuf", [1, 2, 1024])
            S.dma("sp", rowbuf[0:1, 0, :], bmod_d[2048:3072].rearrange("(o n) -> o n", o=1), "rowbuf", w=[rowbuf])
            S.dma("sp", rowbuf[0:1, 1, :], fg_d.rearrange("(o n) -> o n", o=1), "rowbuf", w=[rowbuf])
            wm = [sbt(s1, "wm%d" % i, [128, 8, 512]) for i in range(2)]
            gtrow = sbt(s1, "gtrow", [1, 1024])
            for j in range(6):
                wb = wm[j % 2]
                S.dma("sp", wb[:], wmod_d[:, j * 512:(j + 1) * 512].rearrange("(k p) n -> p k n", p=128),
                      "wm%d" % (j % 2), w=[wb])
                if j < 4:
                    for mi in range(4):
                        m = j * 4 + mi
                        pt = PS[m % 2]
                        for kc in range(8):
                            S.op("pe", lambda e, kc=kc, mi=mi, wb=wb, pt=pt: e.matmul(
                                pt[:, 0:2], lhsT=wb[:, kc, mi * 128:(mi + 1) * 128], rhs=scT[:, kc, :],
                                start=(kc == 0), stop=(kc == 7)), r=[wb, scT], w=[pt])
                        S.op("dve", lambda e, m=m, pt=pt: e.tensor_scalar(
                            out=modc[:, m, :], in0=pt[:, 0:2], scalar1=bmodc[:, m:m + 1], scalar2=None, op0=ALU.add),
                            r=[pt, bmodc], w=[modc])
                else:
                    pt = PS[2 + (j % 2)]
                    for kc in range(8):
                        S.op("pe", lambda e, kc=kc, wb=wb, pt=pt: e.matmul(
                            pt[0:1, :], lhsT=scT[:, kc, 0:1], rhs=wb[:, kc, :], start=(kc == 0), stop=(kc == 7)),
                            r=[wb, scT], w=[pt])
                    S.op("dve", lambda e, j=j, pt=pt: e.tensor_tensor(
                        out=gtrow[0:1, (j - 4) * 512:(j - 3) * 512], in0=pt[0:1, :],
                        in1=rowbuf[0:1, 0, (j - 4) * 512:(j - 3) * 512], op=ALU.add), r=[pt, rowbuf], w=[gtrow])
            for h in range(2):
                pt = PS[4 + h]
                S.op("pe", lambda e, h=h, pt=pt: e.matmul(pt[:], lhsT=ones[0:1, :], rhs=gtrow[0:1, h * 512:(h + 1) * 512],
                                                         start=True, stop=True), r=[ones, gtrow], w=[pt])
                S.op("act", lambda e, h=h, pt=pt: e.activation(out=gtbc[:, h * 512:(h + 1) * 512], in_=pt[:],
                                                               func=AF.Identity), r=[pt], w=[gtbc])
                pt2 = PS[h]
                S.op("pe", lambda e, h=h, pt2=pt2: e.matmul(pt2[:], lhsT=ones[0:1, :],
                                                            rhs=rowbuf[0:1, 1, h * 512:(h + 1) * 512],
                                                            start=True, stop=True), r=[ones, rowbuf], w=[pt2])
                S.op("act", lambda e, h=h, pt2=pt2: e.activation(out=fgbc[:, h * 512:(h + 1) * 512], in_=pt2[:],
                                                                 func=AF.Identity), r=[pt2], w=[fgbc])
            S.op("dve", lambda e: e.tensor_scalar(out=Acol[:], in0=modc[:, 8:16, :], scalar1=1.0, scalar2=None,
                                                  op0=ALU.add), r=[modc], w=[Acol])
            for i in range(2):
                S.op("dve", lambda e, i=i: e.tensor_tensor(out=Acol[:, :, i], in0=Acol[:, :, i], in1=ngc[:],
                                                           op=ALU.mult), r=[Acol, ngc], w=[Acol])

            xt = [sbt(s1, "xt%d" % i, [128, 1024]) for i in range(2)]
            xn = [sbt(s1, "xn%d" % i, [128, 1024]) for i in range(2)]
            junk = sbt(s1, "junk", [128, 1024], BF16)
            ssq = sbt(s1, "ssq", [128, 4])
            hT = [sbt(s1, "hT%d" % i, [128, 8, 512], BF16) for i in range(2)]
            stg = [sbt(s1, "stg%d" % i, [128, 512], BF16) for i in range(4)]
            stg32 = [sbt(s1, "stgf%d" % i, [128, 512]) for i in range(2)]
            groups = [(0, 2)] + [(2 + 4 * g, 4) for g in range(4)]
            nstg = [0, 0, 0, 0]
            def prep_gen(gi):
                tile0, ntl = groups[gi]
                hb = hT[gi % 2]
                ci = 1 if gi == 0 else 0

                def stage1(tl):
                    tile = tile0 + tl
                    xb_, xnb = xt[tile % 2], xn[tile % 2]
                    src = ctx_d[tile * 128:(tile + 1) * 128, :] if tile < 2 else x_d[(tile - 2) * 128:(tile - 1) * 128, :]
                    S.dma("sp", xb_[:], src, xb_.name, w=[xb_])
                    S.op("act", lambda e: e.activation(out=junk[:], in_=xb_[:], func=AF.Square,
                                                       accum_out=ssq[:, 0:1]), r=[xb_], w=[junk, ssq])
                    S.op("dve", lambda e: e.tensor_scalar(out=ssq[:, 1:2], in0=ssq[:, 0:1], scalar1=1.0 / D, scalar2=EPS,
                                                          op0=ALU.mult, op1=ALU.add), r=[ssq], w=[ssq])
                    S.op("act", lambda e: e.activation(out=ssq[:, 2:3], in_=ssq[:, 1:2], func=AF.Ln), r=[ssq], w=[ssq])
                    S.op("act", lambda e: e.activation(out=ssq[:, 3:4], in_=ssq[:, 2:3], func=AF.Exp, scale=-0.5),
                         r=[ssq], w=[ssq])
                    S.op("dve", lambda e: e.tensor_scalar(out=xnb[:], in0=xb_[:], scalar1=ssq[:, 3:4],
                                                          scalar2=None, op0=ALU.mult), r=[xb_, ssq], w=[xnb])

                def stage2(tl):
                    tile = tile0 + tl
                    xnb = xn[tile % 2]
                    for kc in range(8):
                        pt = PS[kc // 4]
                        S.op("pe", lambda e, kc=kc, pt=pt: e.transpose(
                            out=pt[:, (kc % 4) * 128:(kc % 4 + 1) * 128], in_=xnb[:, kc * 128:(kc + 1) * 128],
                            identity=ident[:]), r=[xnb, ident], w=[pt])
                    for kc in range(8):
                        pt = PS[kc // 4]
                        S.op("act", lambda e, kc=kc, pt=pt: e.activation(
                            out=hb[:, kc, tl * 128:(tl + 1) * 128], in_=pt[:, (kc % 4) * 128:(kc % 4 + 1) * 128],
                            func=AF.Identity, scale=Acol[:, kc, ci:ci + 1], bias=modc[:, kc, ci:ci + 1]),
                            r=[pt, Acol, modc], w=[hb])
                seq = [(stage1, 0)]
                for tl in range(ntl):
                    if tl + 1 < ntl:
                        seq.append((stage1, tl + 1))
                    seq.append((stage2, tl))
                for fn_, a_ in seq:
                    fn_(a_)
                    yield

            fm = []
            for i in range(2):
                fm.append((C_AQ + i * 128, 128, s_qTa, i * 128, "cp", None, True))
                fm.append((C_AK + i * 128, 128, s_kTa, i * 128, "cp", None, True))
            fm.append((C_ALR, 32, s_alr, 0, "cp32", None, True))
            for i in range(12):
                fm.append((C_BQ + i * 128, 128, s_zg, i * 128, "cp", None, True))
            for i in range(4):
                fm.append((C_AG + i * 128, 128, s_ga, i * 128, "silu", None, False))
                fm.append((C_BG + i * 128, 128, s_gb, i * 128, "silu", None, False))
            for i in range(16):
                fm.append((C_MG + i * 128, 128, s_gm, i * 128, "sig", i, False))

            def fm_block(gi, spec):
                (c0, ncol, dst, r0, kind, bi, withctx) = spec
                tile0, ntl = groups[gi]
                hb = hT[gi % 2]; ntok = ntl * 128; tok0 = tile0 * 128
                pt = PS[2 + nstg[2] % 3]; nstg[2] += 1
                for kc in range(8):
                    S.op("pe", lambda e, kc=kc: e.matmul(
                        pt[0:ncol, 0:ntok], lhsT=win[:, kc, c0:c0 + ncol], rhs=hb[:, kc, 0:ntok],
                        start=(kc == 0), stop=(kc == 7)), r=[win, hb], w=[pt])
                if kind == "cp32":
                    sg = stg32[nstg[1] % 2]; nstg[1] += 1
                    S.op("dve", lambda e: e.tensor_copy(out=sg[0:ncol, 0:ntok], in_=pt[0:ncol, 0:ntok]), r=[pt], w=[sg])
                else:
                    sg = stg[nstg[0] % 4]; nstg[0] += 1
                    if kind == "cp":
                        S.op("dve", lambda e: e.tensor_copy(out=sg[0:ncol, 0:ntok], in_=pt[0:ncol, 0:ntok]),
                             r=[pt], w=[sg])
                    elif kind == "silu":
                        S.op("act", lambda e: e.activation(out=sg[0:ncol, 0:ntok], in_=pt[0:ncol, 0:ntok],
                                                           func=AF.Silu), r=[pt], w=[sg])
                    else:
                        S.op("act", lambda e: e.activation(out=sg[0:ncol, 0:ntok], in_=pt[0:ncol, 0:ntok],
                                                           func=AF.Sigmoid, bias=bgc[:, bi:bi + 1]),
                             r=[pt, bgc], w=[sg])
                cofs = tok0 if withctx else tok0 - TC
                S.dma("sp", dst[r0:r0 + ncol, cofs:cofs + ntok], sg[0:ncol, 0:ntok], sg.name, r=[sg],
                      w=[dst.tensor.name + "_w"])

            def tm_block(gi, tl, spec):
                (c0, ncol, dst, f32) = spec
                tile0, ntl = groups[gi]
                hb = hT[gi % 2]; tile = tile0 + tl
                pt = PS[5 + nstg[3] % 2]; nstg[3] += 1
                for kc in range(8):
                    S.op("pe", lambda e, kc=kc: e.matmul(
                        pt[:, 0:ncol], lhsT=hb[:, kc, tl * 128:(tl + 1) * 128], rhs=win[:, kc, c0:c0 + ncol],
                        start=(kc == 0), stop=(kc == 7)), r=[win, hb], w=[pt])
                if f32:
                    sg = stg32[nstg[1] % 2]; nstg[1] += 1
                else:
                    sg = stg[nstg[0] % 4]; nstg[0] += 1
                S.op("act", lambda e: e.activation(out=sg[:, 0:ncol], in_=pt[:, 0:ncol], func=AF.Identity),
                     r=[pt], w=[sg])
                S.dma("sp", dst[tile * 128:(tile + 1) * 128, :], sg[:, 0:ncol], sg.name, r=[sg],
                      w=[dst.tensor.name + "_w"])

            for _ in prep_gen(0):
                pass
            for gi, (tile0, ntl) in enumerate(groups):
                nxt = prep_gen(gi + 1) if gi + 1 < len(groups) else None
                blocks = [(fm_block, (gi, sp_)) for sp_ in fm if (gi > 0 or sp_[6])]
                for tl in range(ntl):
                    for sp_ in ((C_AK, 256, s_kta, False), (C_AV, 512, s_vta, False), (C_BB, 16, s_bd, True)):
                        blocks.append((tm_block, (gi, tl, sp_)))
                every = max(1, (len(blocks) - 2) // 10)
                for bi_, (fn_, args_) in enumerate(blocks):
                    fn_(*args_)
                    if nxt is not None and bi_ % every == every - 1:
                        next(nxt, None)
                if nxt is not None:
                    for _ in nxt:
                        pass
            _barrier(S)

        if debug == 1:
            S.finish()
            return nc
        def mm(out, lhsT, rhs, st_=True, sp_=True):
            S.op("pe", lambda e: e.matmul(out, lhsT=lhsT, rhs=rhs, start=st_, stop=sp_), r=[lhsT, rhs], w=[out])

        def tr(out, in_, idn):
            S.op("pe", lambda e: e.transpose(out=out, in_=in_, identity=idn), r=[in_, idn], w=[out])

        def act(out, in_, func, bias=None, scale=None):
            kw = {}
            rr = [in_]
            if bias is not None:
                kw["bias"] = bias
                if not isinstance(bias, float):
                    rr.append(bias)
            if scale is not None:
                kw["scale"] = scale
                if not isinstance(scale, float):
                    rr.append(scale)
            S.op("act", lambda e: e.activation(out=out, in_=in_, func=func, **kw), r=rr, w=[out])

        def tt(eng, out, in0, in1, op):
            S.op(eng, lambda e: e.tensor_tensor(out=out, in0=in0, in1=in1, op=op), r=[in0, in1], w=[out])

        def ts(eng, out, in0, s1, op0, s2=None, op1=None):
            rr = [in0] + [v for v in (s1, s2) if v is not None and not isinstance(v, float)]
            if op1 is None:
                S.op(eng, lambda e: e.tensor_scalar(out=out, in0=in0, scalar1=s1, scalar2=None, op0=op0), r=rr, w=[out])
            else:
                S.op(eng, lambda e: e.tensor_scalar(out=out, in0=in0, scalar1=s1, scalar2=s2, op0=op0, op1=op1),
                     r=rr, w=[out])

        def stt(out, in0, scalar, in1, op0, op1):
            rr = [in0, in1] + ([] if isinstance(scalar, float) else [scalar])
            S.op("dve", lambda e: e.scalar_tensor_tensor(out=out, in0=in0, scalar=scalar, in1=in1, op0=op0, op1=op1),
                 r=rr, w=[out])

        def cp(eng, out, in_):
            if eng == "act":
                act(out, in_, AF.Identity)
            else:
                S.op(eng, lambda e: e.tensor_copy(out=out, in_=in_), r=[in_], w=[out])

        def rsqrt_(out, in_, eps=0.0):
            act(out, in_, AF.Ln, bias=eps)
            act(out, out, AF.Exp, scale=-0.5)

        with ExitStack() as s2:
            PS = [s2.enter_context(nc.psum_tensor("psd%d" % i, [128, 512], F32)) for i in range(6)]
            PSBs = [s2.enter_context(nc.psum_tensor("psbf%d" % i, [128, 1024], BF16)) for i in range(2)]
            cwc = sbt(s2, "cwc", [128, 12, 9])
            for cb in range(12):
                S.dma("sp", cwc[:, cb, :], conv_d[:, cb * 128:(cb + 1) * 128].rearrange("t p -> p t"), "cwc", w=[cwc],
                      allow_slow_non_contiguous=True)

            def d_chain(par):
                n = "d%d" % par
                pl = sbt(s2, "padl" + n, [128, 34, 66], BF16); pc = sbt(s2, "padc" + n, [128, 258], BF16)
                S.op("pool", lambda e: e.memset(pl[:], 0.0), w=[pl])
                S.op("pool", lambda e: e.memset(pc[:], 0.0), w=[pc])
                dgb = sbt(s2, "dg" + n, [128, 9, 128], BF16)
                s32 = sbt(s2, "s32" + n, [128, TT]); sq = sbt(s2, "sq" + n, [128, 512]); ssb = sbt(s2, "ssb" + n, [128, TT])
                nbb = sbt(s2, "nb" + n, [128, TT], BF16)
                tstg = [sbt(s2, "tstg%d" % i + n, [128, 1024], BF16) for i in range(2)]
                pcv = [PS[3 * par], PS[3 * par + 1]]; pnm = PS[3 * par + 2]; PSB = PSBs[par]
                npd = 0
                for cb in range(par, 12, 2):
                    S.dma("sp", pc[:, 1:257], s_zg[cb * 128:(cb + 1) * 128, 0:256], pc.name, r=["s_zg_w"], w=[pc])
                    S.dma("sp", pl[:, 1:33, 1:65],
                          s_zg[cb * 128:(cb + 1) * 128, 256:TT].rearrange("p (r c) -> p r c", c=64),
                          pl.name, r=["s_zg_w"], w=[pl])
                    tt("dve", dgb[:], identb[:, :].unsqueeze(1).to_broadcast([128, 9, 128]),
                       cwc[:, cb, :].unsqueeze(2).to_broadcast([128, 9, 128]), ALU.mult)
                    yield
                    isv = cb >= 8
                    dsil = nbb if isv else s32
                    pt = pcv[npd % 2]; npd += 1
                    for dx in range(3):
                        mm(pt[:, 0:256], dgb[:, 3 + dx, :], pc[:, dx:dx + 256], dx == 0, dx == 2)
                    act(dsil[:, 0:256], pt[:, 0:256], AF.Silu)
                    yield
                    for g in range(4):
                        pt = pcv[npd % 2]; npd += 1
                        for tap in range(9):
                            dy, dx = divmod(tap, 3)
                            mm(pt[:, :].rearrange("p (r c) -> p r c", c=64), dgb[:, tap, :],
                               pl[:, 8 * g + dy:8 * g + dy + 8, dx:dx + 64], tap == 0, tap == 8)
                        act(dsil[:, 256 + 512 * g:256 + 512 * (g + 1)], pt[:, :], AF.Silu)
                        yield
                    if cb < 8:
                        scl = (128.0 ** -0.5) if cb < 4 else 1.0
                        for (c0, n_) in [(0, 256)] + [(256 + 512 * g, 512) for g in range(4)]:
                            act(sq[:, 0:n_], s32[:, c0:c0 + n_], AF.Square)
                            mm(pnm[:, 0:n_], ones[:], sq[:, 0:n_])
                            ts("dve", ssb[:, c0:c0 + n_], pnm[:, 0:n_], EPS, ALU.add)
                            yield
                        act(ssb[:], ssb[:], AF.Ln)
                        act(ssb[:], ssb[:], AF.Exp, scale=-0.5)
                        yield
                        for (c0, n_) in [(0, 1280), (1280, 1024)]:
                            stt(nbb[:, c0:c0 + n_], s32[:, c0:c0 + n_], scl, ssb[:, c0:c0 + n_], ALU.mult, ALU.mult)
                            yield
                        dst = s_qTb if cb < 4 else s_kTb
                        S.dma("sp", dst[(cb % 4) * 128:(cb % 4 + 1) * 128, :], nbb[:], nbb.name, r=[nbb],
                              w=[dst.tensor.name + "_w"])
                    if cb >= 4:
                        dstT = s_ktb if cb < 8 else s_vtb
                        for (t0_, nt_) in ((0, 8), (8, 8), (16, 2)):
                            tb = tstg[npd % 2]; npd += 1
                            for j in range(nt_):
                                tr(PSB[:, j * 128:(j + 1) * 128], nbb[:, (t0_ + j) * 128:(t0_ + j + 1) * 128], identb[:])
                            cp("dve", tb[:, 0:nt_ * 128], PSB[:, 0:nt_ * 128])
                            S.dma("sp", dstT[t0_ * 128:(t0_ + nt_) * 128, (cb % 4) * 128:(cb % 4 + 1) * 128].rearrange(
                                "(t p) c -> p t c", p=128), tb[:, 0:nt_ * 128].rearrange("p (t c) -> p t c", c=128),
                                tb.name, r=[tb], w=[dstT.tensor.name + "_w"])
                            yield
            run_chains([d_chain(0), d_chain(1)], [0, 3])
            _barrier(S)
        if debug == 2:
            S.finish()
            return nc

        with ExitStack() as s3:
            oacc = sbt(s3, "oacc", [128, 16, 8, 128])
            with ExitStack() as s4:
                PE_ = [s4.enter_context(nc.psum_tensor("pse%d" % i, [128, 512], F32)) for i in range(8)]

                def mk(name, val, fwd_expr, op, fill, n4=4, dt=F32):
                    tl_ = sbt(s4, name, [128, n4, 128], dt)
                    S.op("pool", lambda e: e.memset(tl_[:], val), w=[tl_])
                    S.op("pool", lambda e: e.affine_select(out=tl_[:], in_=tl_[:], pattern=[[0, n4], [fwd_expr, 128]],
                                                           compare_op=op, fill=fill, base=0,
                                                           channel_multiplier=-fwd_expr), r=[tl_], w=[tl_])
                    return tl_
                triS = [mk("triS0", -1.0 / 16, 1, ALU.is_ge, 0.0, 1), mk("triS1", -1.0 / 16, -1, ALU.is_ge, 0.0, 1)]
                triX = [mk("triX0", -1.0 / 16, -1, ALU.is_gt, 0.0, 1), mk("triX1", -1.0 / 16, 1, ALU.is_gt, 0.0, 1)]
                maskI4 = [mk("maskI0", 1.0, 1, ALU.is_ge, 0.0), mk("maskI1", 1.0, -1, ALU.is_ge, 0.0)]
                posL4 = [mk("posL0", 0.0, -1, ALU.is_gt, 30000.0, dt=BF16), mk("posL1", 0.0, 1, ALU.is_gt, 30000.0, dt=BF16)]
                negE4 = [mk("negE0", 0.0, 1, ALU.is_ge, -30000.0, dt=BF16), mk("negE1", 0.0, -1, ALU.is_ge, -30000.0, dt=BF16)]
                onesb = sbt(s4, "onesb", [128, 128], BF16)
                S.op("pool", lambda e: e.memset(onesb[:], 1.0), w=[onesb])
                upx = sbt(s4, "upx", [33, 2, 256])
                S.op("pool", lambda e: e.memset(upx[:], 0.0), w=[upx])
                for d in range(2):
                    S.dma("sp", upx[d * 16:(d + 1) * 16, d, :], gup_d[d], "upx", w=[upx])
                    S.dma("sp", upx[32:33, d, :], gub_d[d:d + 1, :], "upx", w=[upx])
                prow = sbt(s4, "prow", [1, 16])
                S.dma("sp", prow[0:1, 0:8], alog_d.rearrange("(o n) -> o n", o=1), "prow", w=[prow])
                S.dma("sp", prow[0:1, 8:16], dtb_d.rearrange("(o n) -> o n", o=1), "prow", w=[prow])
                pbc = sbt(s4, "pbc", [128, 16])
                mm(PE_[0][:, 0:16], ones[0:1, :], prow[0:1, :])
                cp("dve", pbc[:], PE_[0][:, 0:16])
                act(pbc[:, 0:8], pbc[:, 0:8], AF.Exp)
                ts("dve", pbc[:, 0:8], pbc[:, 0:8], -1.0, ALU.mult)
                bdall = sbt(s4, "bdall", [128, NTILE, 16])
                S.dma("sp", bdall[:], s_bd.rearrange("(t p) c -> p t c", p=128), "bdall", r=["s_bd_w"], w=[bdall])
                LNBall = sbt(s4, "LNBall", [128, NTILE, 8]); BETAall = sbt(s4, "BETAall", [128, NTILE, 8])
                GGall = sbt(s4, "GGall", [128, NTILE, 8])
                act(LNBall[:], bdall[:, :, 0:8], AF.Exp, scale=-1.0)
                act(LNBall[:], LNBall[:], AF.Ln, bias=1.0)
                act(BETAall[:], LNBall[:], AF.Exp, scale=-1.0)
                ts("dve", LNBall[:], LNBall[:], -1.0, ALU.mult)
                tt("dve", GGall[:], bdall[:, :, 8:16], pbc[:, 8:16].unsqueeze(1).to_broadcast([128, NTILE, 8]), ALU.add)
                act(GGall[:], GGall[:], AF.Exp)
                act(GGall[:], GGall[:], AF.Ln, bias=1.0)
                tt("dve", GGall[:], GGall[:], pbc[:, 0:8].unsqueeze(1).to_broadcast([128, NTILE, 8]), ALU.mult)
                def gcol(nm, dt=F32):
                    return sbt(s4, nm, [128, NTILE, 8], dt)
                GAMall = gcol("GAMall"); GLall = gcol("GLall"); CLall = gcol("CLall"); NGAMall = gcol("NGAMall")
                EGAMall = gcol("EGAMall"); BEKall = gcol("BEKall"); EKDall = gcol("EKDall"); EDECall = gcol("EDECall")
                GB16all = gcol("GB16all", BF16); GLOall = gcol("GLOall")
                for t_ in range(NTILE):
                    for d_ in range(2):
                        c_ = t_ * 8 + d_ * 4
                        mm(PE_[0][:, c_:c_ + 4], maskI4[d_][:, 0, :], GGall[:, t_, d_ * 4:(d_ + 1) * 4])
                        mm(PE_[1][:, c_:c_ + 4], ones[:], GGall[:, t_, d_ * 4:(d_ + 1) * 4])
                fl = lambda a: a[:].rearrange("p a b -> p (a b)")
                cp("dve", fl(GAMall), PE_[0][:, 0:NTILE * 8])
                cp("act", fl(GLall), PE_[1][:, 0:NTILE * 8])
                tt("dve", CLall[:], LNBall[:], GAMall[:], ALU.add)
                ts("dve", NGAMall[:], GAMall[:], -1.0, ALU.mult)
                tt("dve", EKDall[:], GLall[:], GAMall[:], ALU.subtract)
                act(EGAMall[:], GAMall[:], AF.Exp)
                act(EKDall[:], EKDall[:], AF.Exp)
                act(EDECall[:], GLall[:], AF.Exp)
                tt("dve", BEKall[:], BETAall[:], EGAMall[:], ALU.mult)
                cp("dve", GB16all[:], GGall[:])
                tt("dve", GLOall[:], GGall[:], GB16all[:], ALU.subtract)
                hm = sbt(s4, "hm", [128, 2])
                S.op("pool", lambda e: e.memset(hm[:], 0.0), w=[hm])
                S.op("pool", lambda e: e.memset(hm[0:64, 0:1], 1.0), r=[hm], w=[hm])
                S.op("pool", lambda e: e.memset(hm[64:128, 1:2], 1.0), r=[hm], w=[hm])
                owritten = set()

                def owrite(t, h0, pbank):
                    ov = oacc[:, t - 2, h0:h0 + 4, :]
                    pv = pbank[:, :].rearrange("p (a b) -> p a b", b=128)
                    if (t, h0) not in owritten:
                        owritten.add((t, h0))
                        cp("dve", ov, pv)
                    else:
                        tt("dve", ov, pv, ov, ALU.add)

                def orders(d):
                    return list(range(NTILE)) if d == 0 else [1, 0] + list(range(NTILE - 1, 1, -1))

                def v3(a):
                    return a[:, :].rearrange("p (a b) -> p a b", b=128)

                def gla_chain(d):
                    n = "a%d" % d
                    pg = PE_[d]
                    def db(nm, shape, dt):
                        return [sbt(s4, nm + n + str(i), shape, dt) for i in range(2)]
                    qTa = db("qTa", [128, 2, 128], BF16); kTa = db("kTa", [128, 2, 128], BF16)
                    kta = db("kta", [128, 256], BF16); vta = db("vta", [128, 512], BF16); alrx = db("alrx", [33, 128], F32)
                    for i in range(2):
                        S.op("pool", lambda e, i=i: e.memset(alrx[i][:], 1.0), w=[alrx[i]])
                    l1 = sbt(s4, "l1" + n, [128, 256]); ebT = sbt(s4, "ebT" + n, [128, 256]); enb = sbt(s4, "enb" + n, [128, 256])
                    qg = sbt(s4, "qg" + n, [128, 2, 128], BF16); kg = sbt(s4, "kg" + n, [128, 2, 128], BF16)
                    qgm = sbt(s4, "qgm" + n, [128, 4, 128], BF16)
                    ekd = sbt(s4, "ekd" + n, [128, 256]); kd = sbt(s4, "kd" + n, [128, 256], BF16)
                    attm = sbt(s4, "attm" + n, [128, 512], BF16)
                    Sa32 = sbt(s4, "Sa32" + n, [128, 2, 128]); Sa16 = sbt(s4, "Sa16" + n, [128, 2, 128], BF16)
                    S.op("pool", lambda e: e.memset(Sa32[:], 0.0), w=[Sa32])
                    S.op("pool", lambda e: e.memset(Sa16[:], 0.0), w=[Sa16])
                    order = orders(d)
                    last = 127 if d == 0 else 0

                    def load(t, bi):
                        tk = slice(t * 128, (t + 1) * 128)
                        S.dma("sp", qTa[bi][:], s_qTa[:, tk].rearrange("(h p) t -> p h t", p=128), qTa[bi].name, r=["s_qTa_w"], w=[qTa[bi]])
                        S.dma("sp", kTa[bi][:], s_kTa[:, tk].rearrange("(h p) t -> p h t", p=128), kTa[bi].name, r=["s_kTa_w"], w=[kTa[bi]])
                        S.dma("sp", kta[bi][:], s_kta[tk, :], kta[bi].name, r=["s_kta_w"], w=[kta[bi]])
                        S.dma("sp", vta[bi][:], s_vta[tk, :], vta[bi].name, r=["s_vta_w"], w=[vta[bi]])
                        S.dma("sp", alrx[bi][0:32, :], s_alr[:, tk], alrx[bi].name, r=["s_alr_w"], w=[alrx[bi]])
                    load(order[0], 0)
                    for si, t in enumerate(order):
                        bi = si % 2
                        if si + 1 < len(order):
                            load(order[si + 1], (si + 1) % 2)
                        lat = t >= 2
                        q_a, k_a, kt_a, vt_a, al_ = qTa[bi], kTa[bi], kta[bi], vta[bi], alrx[bi]
                        mm(pg[:, 0:256], al_[0:33, :], upx[0:33, d, :])
                        act(l1[:], pg[:, 0:256], AF.Exp, scale=-1.0)
                        act(l1[:], l1[:], AF.Ln, bias=1.0)
                        yield
                        for hp in range(2):
                            mm(pg[:, 256 + hp * 128:256 + (hp + 1) * 128], l1[:, hp * 128:(hp + 1) * 128], triS[d][:, 0, :])
                        act(ebT[:], pg[:, 256:512], AF.Exp)
                        act(enb[:], pg[:, 256:512], AF.Exp, scale=-1.0)
                        yield
                        mm(pg[:, 0:256], triX[d][:, 0, :], l1[:])
                        stt(qg[:].rearrange("p a b -> p (a b)"), q_a[:].rearrange("p a b -> p (a b)"), 0.125, ebT[:],
                            ALU.mult, ALU.mult)
                        tt("dve", kg[:].rearrange("p a b -> p (a b)"), k_a[:].rearrange("p a b -> p (a b)"), enb[:], ALU.mult)
                        act(ekd[:], pg[:, 0:256], AF.Exp)
                        tt("dve", kd[:], kt_a[:], ekd[:], ALU.mult)
                        yield
                        if lat:
                            for hp in range(2):
                                tt("pool", qgm[:, 2 * hp:2 * hp + 2, :], qg[:, hp:hp + 1, :].to_broadcast([128, 2, 128]),
                                   hm[:, :].unsqueeze(2).to_broadcast([128, 2, 128]), ALU.mult)
                            for h in range(4):
                                mm(pg[:, h * 128:(h + 1) * 128], kg[:, h // 2, :], qgm[:, h, :])
                            tt("dve", attm[:], pg[:, :], maskI4[d][:].rearrange("p a b -> p (a b)"), ALU.mult)
                            yield
                            for h in range(4):
                                mm(pg[:, h * 128:(h + 1) * 128], vt_a[:, h * 128:(h + 1) * 128],
                                   attm[:, h * 128:(h + 1) * 128], True, False)
                                mm(pg[:, h * 128:(h + 1) * 128], Sa16[:, h // 2, :], qgm[:, h, :], False, True)
                            owrite(t, 0, pg)
                            yield
                        for hp in range(2):
                            mm(pg[:, 0:256], kd[:, hp * 128:(hp + 1) * 128], vt_a[:, hp * 256:(hp + 1) * 256])
                            for hf in range(2):
                                rsl = slice(hf * 64, (hf + 1) * 64)
                                stt(Sa32[rsl, hp, :], Sa32[rsl, hp, :], ebT[rsl, hp * 128 + last:hp * 128 + last + 1],
                                    pg[rsl, hf * 128:(hf + 1) * 128], ALU.mult, ALU.add)
                        cp("act", Sa16[:], Sa32[:])
                        yield

                def gdn_chain(d):
                    n = "b%d" % d
                    pa, pb, pc = PE_[2 + 3 * d], PE_[3 + 3 * d], PE_[4 + 3 * d]
                    def db(nm, shape, dt):
                        return [sbt(s4, nm + n + str(i), shape, dt) for i in range(2)]
                    qTb = db("qTb", [128, 4, 128], BF16); kTb = db("kTb", [128, 4, 128], BF16)
                    ktb = db("ktb", [128, 512], BF16); vtb = db("vtb", [128, 512], BF16); bdt = db("bdt", [128, 16], F32)
                    cols = sbt(s4, "cols" + n, [128, 16, 4])
                    f32t = lambda nm: sbt(s4, nm + n, [128, 512])
                    b16t = lambda nm: sbt(s4, nm + n, [128, 512], BF16)
                    rGh = b16t("rGh"); rGl = b16t("rGl"); XA = f32t("XA"); XB = f32t("XB"); Lm = f32t("Lm"); LTm = f32t("LTm")
                    Mx = [f32t("Mx0"), f32t("Mx1")]; Mtx = [f32t("Mtx0"), f32t("Mtx1")]; Pt = f32t("Pt"); u32 = f32t("u32")
                    Tt16 = b16t("Tt16"); vb = b16t("vb"); kbg = b16t("kbg"); kdb = b16t("kdb"); wT16 = b16t("wT16")
                    aqk = b16t("aqk"); qgb = b16t("qgb"); vnew = b16t("vnew")
                    Sb32 = sbt(s4, "Sb32" + n, [128, 4, 128]); Sb16 = sbt(s4, "Sb16" + n, [128, 4, 128], BF16)
                    S.op("pool", lambda e: e.memset(Sb32[:], 0.0), w=[Sb32])
                    S.op("pool", lambda e: e.memset(Sb16[:], 0.0), w=[Sb16])
                    BETA, LNB, YY, GG, GAM, GL, CL, NGAM, EGAM, BEK, EKD, EDEC, GLO = range(13)
                    colv = {}
                    C = lambda i: colv.get(i, cols[:, i, :])
                    CB = lambda i: C(i).unsqueeze(2).to_broadcast([128, 4, 128])
                    order = orders(d)

                    def load(t, bi):
                        tk = slice(t * 128, (t + 1) * 128)
                        S.dma("sp", qTb[bi][:], s_qTb[:, tk].rearrange("(h p) t -> p h t", p=128), qTb[bi].name, r=["s_qTb_w"], w=[qTb[bi]])
                        S.dma("sp", kTb[bi][:], s_kTb[:, tk].rearrange("(h p) t -> p h t", p=128), kTb[bi].name, r=["s_kTb_w"], w=[kTb[bi]])
                        S.dma("sp", ktb[bi][:], s_ktb[tk, :], ktb[bi].name, r=["s_ktb_w"], w=[ktb[bi]])
                        S.dma("sp", vtb[bi][:], s_vtb[tk, :], vtb[bi].name, r=["s_vtb_w"], w=[vtb[bi]])
                    load(order[0], 0)
                    for si, t in enumerate(order):
                        bi = si % 2
                        if si + 1 < len(order):
                            load(order[si + 1], (si + 1) % 2)
                        lat = t >= 2
                        q_b, k_b, kt_b, vt_b, bd_ = qTb[bi], kTb[bi], ktb[bi], vtb[bi], bdt[bi]
                        gsl = slice(d * 4, (d + 1) * 4)
                        colv = {BETA: BETAall[:, t, gsl], LNB: LNBall[:, t, gsl], GG: GGall[:, t, gsl],
                                GAM: GAMall[:, t, gsl], GL: GLall[:, t, gsl], CL: CLall[:, t, gsl],
                                NGAM: NGAMall[:, t, gsl], EGAM: EGAMall[:, t, gsl], BEK: BEKall[:, t, gsl],
                                EKD: EKDall[:, t, gsl], EDEC: EDECall[:, t, gsl], GLO: GLOall[:, t, gsl]}
                        gb16 = GB16all[:, t, gsl]
                        tt("pool", v3(rGh), maskI4[d][:], gb16.unsqueeze(2).to_broadcast([128, 4, 128]), ALU.mult)
                        tt("pool", v3(rGl), maskI4[d][:], CB(GLO), ALU.mult)
                        tt("pool", Sb32[:], Sb32[:], CB(EDEC), ALU.mult)
                        yield
                        mm(pa[:, :], onesb[:], rGh[:], True, False)
                        mm(pa[:, :], onesb[:], rGl[:], False, False)
                        mm(pa[:, :], identb[:], posL4[d][:].rearrange("p a b -> p (a b)"), False, True)
                        for h in range(4):
                            mm(pb[:, h * 128:(h + 1) * 128], k_b[:, h, :], k_b[:, h, :])
                        for h in range(4):
                            act(XA[:, h * 128:(h + 1) * 128], pa[:, h * 128:(h + 1) * 128], AF.Exp,
                                bias=C(CL)[:, h:h + 1], scale=-1.0)
                        yield
                        tt("dve", Lm[:], pb[:, :], XA[:], ALU.mult)
                        tt("pool", v3(vb), v3(vt_b), CB(BETA), ALU.mult)
                        yield
                        for h in range(4):
                            tr(pb[:, h * 128:(h + 1) * 128], Lm[:, h * 128:(h + 1) * 128], ident[:])
                        if lat:
                            mm(pa[:, :], onesb[:], rGh[:], True, False)
                            mm(pa[:, :], onesb[:], rGl[:], False, False)
                            mm(pa[:, :], identb[:], negE4[d][:].rearrange("p a b -> p (a b)"), False, True)
                            for h in range(4):
                                mm(pc[:, h * 128:(h + 1) * 128], k_b[:, h, :], q_b[:, h, :])
                        cp("act", LTm[:], pb[:, :])
                        yield
                        tt("dve", v3(Pt), ident[:, :].unsqueeze(1).to_broadcast([128, 4, 128]), v3(pb), ALU.subtract)
                        if lat:
                            for h in range(4):
                                act(XB[:, h * 128:(h + 1) * 128], pa[:, h * 128:(h + 1) * 128], AF.Exp,
                                    bias=C(NGAM)[:, h:h + 1])
                            yield
                            tt("dve", aqk[:], pc[:, :], XB[:], ALU.mult)
                            mm(pa[:, :], onesb[:], rGh[:], True, False)
                            mm(pa[:, :], onesb[:], rGl[:], False, True)
                            act(XA[:], pa[:, :], AF.Exp)
                            yield
                            tt("pool", qgb[:], q_b[:].rearrange("p a b -> p (a b)"), XA[:], ALU.mult)
                        Mc, Mtc = Lm, LTm
                        for lev in range(6):
                            Mn, Mtn = Mx[lev % 2], Mtx[lev % 2]
                            for h in range(4):
                                hs = slice(h * 128, (h + 1) * 128)
                                mm(pa[:, hs], Mtc[:, hs], Mc[:, hs])
                            if lev < 5:
                                for h in range(4):
                                    hs = slice(h * 128, (h + 1) * 128)
                                    mm(pc[:, hs], Mc[:, hs], Mtc[:, hs])
                            cp("act", Mn[:], pa[:, :])
                            yield
                            if lev < 5:
                                cp("act", Mtn[:], pc[:, :])
                            for h in range(4):
                                hs = slice(h * 128, (h + 1) * 128)
                                mm(pb[:, hs], Mn[:, hs], Pt[:, hs])
                            yield
                            if lev < 5:
                                tt("dve", Pt[:], pb[:, :], Pt[:], ALU.add)
                            else:
                                tt("dve", Tt16[:], pb[:, :], Pt[:], ALU.add)
                            if lev == 2:
                                tt("pool", v3(kbg), v3(kt_b), CB(BEK), ALU.mult)
                            if lev == 3:
                                tt("pool", v3(kdb), v3(kt_b), CB(EKD), ALU.mult)
                            yield
                            Mc, Mtc = Mn, Mtn
                        for h in range(4):
                            hs = slice(h * 128, (h + 1) * 128)
                            mm(pa[:, hs], Tt16[:, hs], vb[:, hs])
                            mm(pc[:, hs], kbg[:, hs], Tt16[:, hs])
                        cp("act", wT16[:], pc[:, :])
                        cp("act", u32[:], pa[:, :])
                        yield
                        for h in range(4):
                            hs = slice(h * 128, (h + 1) * 128)
                            mm(pb[:, hs], wT16[:, hs], Sb16[:, h, :])
                        tt("dve", vnew[:], u32[:], pb[:, :], ALU.subtract)
                        yield
                        if lat:
                            for h in range(4):
                                hs = slice(h * 128, (h + 1) * 128)
                                mm(pa[:, hs], Sb16[:, h, :], qgb[:, hs], True, False)
                                mm(pa[:, hs], vnew[:, hs], aqk[:, hs], False, True)
                        for h in range(4):
                            hs = slice(h * 128, (h + 1) * 128)
                            mm(pc[:, hs], kdb[:, hs], vnew[:, hs])
                        if lat:
                            owrite(t, 4, pa)
                        yield
                        tt("dve", Sb16[:].rearrange("p a b -> p (a b)"), pc[:, :], Sb32[:].rearrange("p a b -> p (a b)"), ALU.add)
                        tt("dve", Sb32[:].rearrange("p a b -> p (a b)"), pc[:, :], Sb32[:].rearrange("p a b -> p (a b)"), ALU.add)
                        yield

                run_chains([gdn_chain(0), gdn_chain(1), gla_chain(0), gla_chain(1)], [0, 12, 0, 8], [1, 1, 3, 3])
                _barrier(S)
            if debug == 3:
                S.dma("sp", s_oacc[:, :], oacc[:].rearrange("p a b c -> p (a b c)"), "dbg", r=[oacc])
                S.finish()
                return nc

            with ExitStack() as s5:
                PS = [s5.enter_context(nc.psum_tensor("psf%d" % i, [128, 512], F32)) for i in range(8)]
                wga = sbt(s5, "wga", [128, 4, 1024], BF16); wgd = sbt(s5, "wgd", [128, 4, 1024], BF16)
                wo = sbt(s5, "wo", [128, 8, 1024], BF16)
                S.dma("pool", wga[:], wga_d.rearrange("(k p) n -> p k n", p=128), "wga", w=[wga])
                S.dma("pool", wgd[:], wgd_d.rearrange("(k p) n -> p k n", p=128), "wgd", w=[wgd])
                S.dma("pool", wo[:], wo_d.rearrange("(k p) n -> p k n", p=128), "wo", w=[wo])
                onc = sbt(s5, "onc", [128, 2])
                S.dma("sp", onc[:, 0:1], gon_d.rearrange("(p o) -> p o", o=1), "onc", w=[onc])
                S.dma("sp", onc[:, 1:2], don_d.rearrange("(p o) -> p o", o=1), "onc", w=[onc])
                om = sbt(s5, "om", [128, 128])
                S.op("pool", lambda e: e.memset(om[:], 1.0 / 128), w=[om])
                gmb = [sbt(s5, "gmb%d" % i, [128, 16, 512], BF16) for i in range(2)]
                yab = [sbt(s5, "yab%d" % i, [128, 8, 512], BF16) for i in range(2)]
                t1 = [sbt(s5, "t1_%d" % i, [128, 512]) for i in range(2)]
                t2 = [sbt(s5, "t2_%d" % i, [128, 512]) for i in range(2)]
                mrg = sbt(s5, "mrg", [128, 8, 512], BF16)
                xr = [sbt(s5, "xr%d" % i, [128, 1024]) for i in range(2)]
                xw = [sbt(s5, "xw%d" % i, [128, 1024]) for i in range(2)]
                jk = sbt(s5, "jk", [128, 1024], BF16)
                fs = sbt(s5, "fs", [128, 4])
                v3 = lambda a: a[:, :].rearrange("p (a b) -> p a b", b=128)

                fprog = {"b": -1, "a0": -1, "a1": -1}

                def f_norm_chain(par):
                    n = "n%d" % par
                    gab = [sbt(s5, "gab0" + n, [128, 512], BF16)] * 2
                    sqf = sbt(s5, "sqf" + n, [128, 512]); rsf = sbt(s5, "rsf" + n, [128, 512])
                    pt = PS[par]
                    k = 0
                    for g in range(4):
                        while fprog["b"] < g - 2:
                            yield
                        tks = slice(g * 512, (g + 1) * 512)
                        if par == 0:
                            S.dma("sp", gmb[g % 2][:], s_gm[:, tks].rearrange("(m p) t -> p m t", p=128),
                                  gmb[g % 2].name, r=["s_gm_w"], w=[gmb[g % 2]])
                        for hd in range(par, 8, 2):
                            gb_ = gab[k % 2]; k += 1
                            src = s_ga if hd < 4 else s_gb
                            S.dma("sp", gb_[:], src[(hd % 4) * 128:(hd % 4 + 1) * 128, tks], gb_.name,
                                  r=[src.tensor.name + "_w"], w=[gb_])
                            ov = oacc[:, 4 * g:4 * g + 4, hd, :]
                            tt("pool", v3(sqf), ov, ov, ALU.mult)
                            mm(pt[:, :], om[:], sqf[:])
                            yield
                            rsqrt_(rsf[:], pt[:, :], EPS)
                            yield
                            tt("pool", v3(sqf), ov, v3(rsf), ALU.mult)
                            yield
                            stt(yab[g % 2][:, hd, :], sqf[:], onc[:, (0 if hd < 4 else 1):(1 if hd < 4 else 2)], gb_[:],
                                ALU.mult, ALU.mult)
                            yield
                        fprog["a%d" % par] = g

                mrg2 = [mrg, sbt(s5, "mrgb", [128, 8, 512], BF16)]
                fprog["p"] = -1; fprog["o"] = -1

                def f_proj_chain():
                    nf = 0
                    for g in range(4):
                        while min(fprog["a0"], fprog["a1"]) < g or fprog["o"] < g - 2:
                            yield
                        ya_, gm_, mg_ = yab[g % 2], gmb[g % 2], mrg2[g % 2]
                        for nb_ in range(8):
                            pa, pb = PS[2 + nf % 2], PS[4 + nf % 2]
                            t1_, t2_ = t1[nf % 2], t2[nf % 2]; nf += 1
                            for kc in range(4):
                                mm(pa[:, :], wga[:, kc, nb_ * 128:(nb_ + 1) * 128], ya_[:, kc, :], kc == 0, kc == 3)
                            for kc in range(4):
                                mm(pb[:, :], wgd[:, kc, nb_ * 128:(nb_ + 1) * 128], ya_[:, 4 + kc, :], kc == 0, kc == 3)
                            yield
                            tt("dve", t1_[:], pa[:, :], gm_[:, nb_, :], ALU.mult)
                            tt("dve", t2_[:], pb[:, :], gm_[:, 8 + nb_, :], ALU.mult)
                            yield
                            tt("pool", mg_[:, nb_, :], t1_[:], t2_[:], ALU.add)
                            yield
                        fprog["b"] = g
                        fprog["p"] = g

                def f_out_chain():
                    for g in range(4):
                        while fprog["p"] < g:
                            yield
                        mg_ = mrg2[g % 2]
                        for tl in range(4):
                            tile = 4 * g + tl
                            xr_, xw_ = xr[tile % 2], xw[tile % 2]
                            S.dma("sp", xr_[:], x_d[tile * 128:(tile + 1) * 128, :], xr_.name, w=[xr_])
                            for c2 in range(2):
                                pt = PS[6 + c2]
                                cs = slice(c2 * 512, (c2 + 1) * 512)
                                for kc in range(8):
                                    mm(pt[:, :], mg_[:, kc, tl * 128:(tl + 1) * 128], wo[:, kc, cs], kc == 0, kc == 7)
                                tt("dve", xw_[:, cs], pt[:, :], gtbc[:, cs], ALU.mult)
                                yield
                            tt("pool", xw_[:], xw_[:], xr_[:], ALU.add)
                            yield
                            S.op("act", lambda e, xw_=xw_: e.activation(out=jk[:], in_=xw_[:], func=AF.Square,
                                                                       accum_out=fs[:, 0:1]), r=[xw_], w=[jk, fs])
                            ts("dve", fs[:, 1:2], fs[:, 0:1], 1.0 / D, ALU.mult)
                            rsqrt_(fs[:, 2:3], fs[:, 1:2], EPS)
                            yield
                            stt(xr_[:], xw_[:], fs[:, 2:3], fgbc[:], ALU.mult, ALU.mult)
                            S.dma("sp", out_d[tile * 128:(tile + 1) * 128, :], xr_[:], xr_.name, r=[xr_], w=["out_w"])
                            yield
                        fprog["o"] = g
                run_chains([f_norm_chain(0), f_norm_chain(1), f_proj_chain(), f_out_chain()], [0, 2, 24, 24])
        S.finish()
    return nc


_NC = {}


def kernel(**inputs):
    nc = _NC.get("nc")
    if nc is None:
        nc = _NC["nc"] = build_program()
    in_maps = make_in_maps(inputs)
    res = run_bass_kernel_spmd(nc, in_maps, core_ids=list(range(8)))
    return np.stack([r["out"] for r in res.results], axis=0).astype(np.float32)


def make_in_maps(inputs):
    f = lambda a: np.ascontiguousarray(np.asarray(a, dtype=np.float32))
    maps = []
    for b in range(8):
        maps.append({
            "x": f(inputs["x"][b]), "c": f(inputs["c"][b]), "ctx": f(inputs["ctx"][b]), "c_ctx": f(inputs["c_ctx"]),
            "w_mod": f(inputs["w_mod"][0]), "b_mod": f(inputs["b_mod"][0]), "norm_g": f(inputs["norm_g"][0]),
            "w_in": f(inputs["w_in"][0]), "gla_up": f(inputs["gla_up"][0]), "gla_ub": f(inputs["gla_ub"][0]),
            "gla_onorm": f(inputs["gla_onorm"][0]), "gdn_conv": f(inputs["gdn_conv"][0]).reshape(9, 1536),
            "gdn_a_log": f(inputs["gdn_a_log"][0]).reshape(8), "gdn_dt_bias": f(inputs["gdn_dt_bias"][0]).reshape(8),
            "gdn_onorm": f(inputs["gdn_onorm"][0]), "w_gla_out": f(inputs["w_gla_out"][0]),
            "w_gdn_out": f(inputs["w_gdn_out"][0]), "b_gate": f(inputs["b_gate"][0]), "w_o": f(inputs["w_o"][0]),
            "final_g": f(inputs["final_g"]),
        })
    return maps
```
